# Optimizing a Trainium2 kernel written in Bass

```python
import math
import jax, jax.numpy as jnp
from jax import lax
import numpy as np

D_MODEL = 1024
BATCH = 16
SEQ = 2048
DEPTH = 4

MLA_HEADS = 6
MLA_Q_RANK = 256
MLA_KV_RANK = 128
MLA_NOPE = 64
MLA_ROPE = 32
MLA_V = 64
MLA_WIDTH = MLA_HEADS * MLA_V
ATTN_BLOCK = 128
RET_HEADS = 4
RET_DK = 32
RET_DV = 64
RET_WIDTH = RET_HEADS * RET_DV
RET_CHUNK = 128
LRU_WIDTH = D_MODEL - MLA_WIDTH - RET_WIDTH
LRU_BLOCKS = 6
LRU_BLOCK = LRU_WIDTH // LRU_BLOCKS
CONV_WIDTH = 4
LRU_C = 8.0
MIX_WIDTH = MLA_WIDTH + RET_WIDTH + LRU_WIDTH
FFN_HIDDEN = -(-8 * D_MODEL // (3 * 256)) * 256
IN_COLS = (MLA_Q_RANK, MLA_KV_RANK, MLA_ROPE,
           RET_HEADS * RET_DK, RET_HEADS * RET_DK, RET_WIDTH, RET_WIDTH,
           LRU_WIDTH, LRU_WIDTH)
D_IN = sum(IN_COLS)
ROPE_BASE = 10000.0
NORM_EPS = 1e-6
N_MOD = 6

kernel_name = "hymba_mla_retention_rglru_adaln"


def rms_norm(x, g):
    xf = x.astype(jnp.float32)
    y = xf * lax.rsqrt(jnp.mean(xf * xf, axis=-1, keepdims=True) + NORM_EPS)
    return (y * g.astype(jnp.float32)).astype(x.dtype)


def rotary(x, pos):
    half = x.shape[-1] // 2
    inv = ROPE_BASE ** (-jnp.arange(half, dtype=jnp.float32) / half)
    ang = pos.astype(jnp.float32)[:, None, :, None] * inv
    cos, sin = jnp.cos(ang), jnp.sin(ang)
    xf = x.astype(jnp.float32)
    x1, x2 = xf[..., :half], xf[..., half:]
    return jnp.concatenate([x1 * cos - x2 * sin, x1 * sin + x2 * cos], axis=-1).astype(x.dtype)


def split_cols(z):
    offsets = np.cumsum(np.array(IN_COLS))[:-1].tolist()
    return jnp.split(z, offsets, axis=-1)


def mla(c_q, c_kv, k_rope, pos, q_norm, w_uq, kv_norm, w_ukv):
    B, S, _ = c_q.shape
    H = MLA_HEADS
    q = (rms_norm(c_q, q_norm) @ w_uq).reshape(B, S, H, MLA_NOPE + MLA_ROPE).transpose(0, 2, 1, 3)
    q = jnp.concatenate([q[..., :MLA_NOPE], rotary(q[..., MLA_NOPE:], pos)], axis=-1)
    kv = (rms_norm(c_kv, kv_norm) @ w_ukv).reshape(B, S, H, MLA_NOPE + MLA_V).transpose(0, 2, 1, 3)
    k_nope, v = kv[..., :MLA_NOPE], kv[..., MLA_NOPE:]
    k_r = rotary(k_rope[:, None], pos)
    k = jnp.concatenate([k_nope, jnp.broadcast_to(k_r, (B, H, S, MLA_ROPE))], axis=-1)
    scale = (MLA_NOPE + MLA_ROPE) ** -0.5
    outs = []
    for blk in range(S // ATTN_BLOCK):
        q0 = blk * ATTN_BLOCK
        q1 = q0 + ATTN_BLOCK
        s = jnp.einsum('bhqd,bhkd->bhqk', q[:, :, q0:q1], k[:, :, :q1]).astype(jnp.float32) * scale
        mask = jnp.arange(q1)[None, :] <= jnp.arange(q0, q1)[:, None]
        s = jnp.where(mask, s, -jnp.inf)
        p = jax.nn.softmax(s, axis=-1).astype(v.dtype)
        outs.append(jnp.einsum('bhqk,bhkd->bhqd', p, v[:, :, :q1]))
    o = jnp.concatenate(outs, axis=2)
    return o.transpose(0, 2, 1, 3).reshape(B, S, H * MLA_V)


def retention(q, k, v, g, pos, gn_gain):
    B, S, _ = q.shape
    H, C = RET_HEADS, RET_CHUNK
    N = S // C
    f32 = jnp.float32
    q = rotary(q.reshape(B, S, H, RET_DK).transpose(0, 2, 1, 3), pos).astype(f32)
    k = rotary(k.reshape(B, S, H, RET_DK).transpose(0, 2, 1, 3), pos).astype(f32) * (RET_DK ** -0.5)
    v = v.reshape(B, S, H, RET_DV).transpose(0, 2, 1, 3).astype(f32)
    log_g = jnp.log(1.0 - jnp.exp2(-5.0 - jnp.arange(H, dtype=f32)))
    idx = jnp.arange(C, dtype=f32)
    diff = idx[:, None] - idx[None, :]
    decay = jnp.where(diff >= 0, jnp.exp(log_g[:, None, None] * jnp.maximum(diff, 0.0)), 0.0)
    q_decay = jnp.exp(log_g[:, None] * (idx + 1.0))
    k_decay = jnp.exp(log_g[:, None] * (C - 1.0 - idx))
    chunk_decay = jnp.exp(log_g * C)
    qc = q.reshape(B, H, N, C, RET_DK)
    kc = k.reshape(B, H, N, C, RET_DK)
    vc = v.reshape(B, H, N, C, RET_DV)
    scores = jnp.einsum('bhncd,bhnmd->bhncm', qc, kc) * decay[None, :, None]
    inner = jnp.einsum('bhncm,bhnme->bhnce', scores, vc)
    kv = jnp.einsum('bhnmd,bhnme->nbhde', kc * k_decay[None, :, None, :, None], vc)

    def step(state, kv_n):
        return state * chunk_decay[None, :, None, None] + kv_n, state

    _, states = lax.scan(step, jnp.zeros((B, H, RET_DK, RET_DV), f32), kv)
    cross = jnp.einsum('bhncd,nbhde->bhnce', qc * q_decay[None, :, None, :, None], states)
    o = (inner + cross).reshape(B, H, S, RET_DV)
    mu = jnp.mean(o, axis=-1, keepdims=True)
    var = jnp.mean(jnp.square(o - mu), axis=-1, keepdims=True)
    o = ((o - mu) * lax.rsqrt(var + NORM_EPS)).transpose(0, 2, 1, 3).reshape(B, S, H * RET_DV)
    o = o * gn_gain.astype(f32)
    return (jax.nn.silu(g.astype(f32)) * o).astype(g.dtype)


def rg_lru_block(xb, gb, conv_w, conv_b, w_a, b_a, w_i, b_i, lam):
    B, S, W = xb.shape
    f32 = jnp.float32
    xc = lax.conv_general_dilated(xb, conv_w[:, None, :], window_strides=(1,),
                                  padding=[(CONV_WIDTH - 1, 0)],
                                  dimension_numbers=('NWC', 'WIO', 'NWC'),
                                  feature_group_count=W) + conv_b
    xg = xc.reshape(B, S, LRU_BLOCKS, LRU_BLOCK)
    r = jax.nn.sigmoid(jnp.einsum('bsgi,gij->bsgj', xg, w_a).reshape(B, S, W) + b_a).astype(f32)
    i = jax.nn.sigmoid(jnp.einsum('bsgi,gij->bsgj', xg, w_i).reshape(B, S, W) + b_i).astype(f32)
    log_a = -LRU_C * r * jax.nn.softplus(-lam.astype(f32))
    a = jnp.exp(log_a)
    b = jnp.sqrt(-jnp.expm1(2.0 * log_a)) * i * xc.astype(f32)

    def combine(left, right):
        a1, b1 = left
        a2, b2 = right
        return a1 * a2, a2 * b1 + b2

    _, h = lax.associative_scan(combine, (a, b), axis=1)
    return (jax.nn.gelu(gb.astype(f32)) * h).astype(xb.dtype)


def setup_inputs(seed: int = 0) -> dict:
    key = jax.random.key(seed)
    ks = iter(jax.random.split(key, 40))
    L, D = DEPTH, D_MODEL

    def nrm(shape, scale):
        return jax.random.normal(next(ks), shape, jnp.float32) * scale

    x = nrm((BATCH, SEQ, D), 1.0)
    c = nrm((BATCH, D), 1.0)
    offset = jax.random.randint(next(ks), (BATCH, 1), 0, 1024, dtype=jnp.int32)
    positions = offset + jnp.arange(SEQ, dtype=jnp.int32)[None, :]
    a0 = jax.random.uniform(next(ks), (L, LRU_WIDTH), jnp.float32, 0.9, 0.999)
    return {
        "x": x,
        "c": c,
        "positions": positions,
        "mod_w": nrm((L, D, N_MOD * D), 0.5 * D ** -0.5),
        "mod_b": nrm((L, N_MOD * D), 0.02),
        "norm1": 1.0 + nrm((L, D), 0.02),
        "w_in": nrm((L, D, D_IN), D ** -0.5),
        "mla_q_norm": 1.0 + nrm((L, MLA_Q_RANK), 0.02),
        "mla_w_uq": nrm((L, MLA_Q_RANK, MLA_HEADS * (MLA_NOPE + MLA_ROPE)), MLA_Q_RANK ** -0.5),
        "mla_kv_norm": 1.0 + nrm((L, MLA_KV_RANK), 0.02),
        "mla_w_ukv": nrm((L, MLA_KV_RANK, MLA_HEADS * (MLA_NOPE + MLA_V)), MLA_KV_RANK ** -0.5),
        "ret_gn": 1.0 + nrm((L, RET_WIDTH), 0.02),
        "lru_conv_w": nrm((L, CONV_WIDTH, LRU_WIDTH), CONV_WIDTH ** -0.5),
        "lru_conv_b": nrm((L, LRU_WIDTH), 0.02),
        "lru_w_a": nrm((L, LRU_BLOCKS, LRU_BLOCK, LRU_BLOCK), LRU_BLOCK ** -0.5),
        "lru_b_a": nrm((L, LRU_WIDTH), 0.02),
        "lru_w_i": nrm((L, LRU_BLOCKS, LRU_BLOCK, LRU_BLOCK), LRU_BLOCK ** -0.5),
        "lru_b_i": nrm((L, LRU_WIDTH), 0.02),
        "lru_lambda": jnp.log(a0) - jnp.log1p(-a0),
        "w_out": nrm((L, MIX_WIDTH, D), MIX_WIDTH ** -0.5),
        "norm2": 1.0 + nrm((L, D), 0.02),
        "w_gate_up": nrm((L, D, 2 * FFN_HIDDEN), D ** -0.5),
        "w_down": nrm((L, FFN_HIDDEN, D), FFN_HIDDEN ** -0.5),
        "final_norm": 1.0 + nrm((D,), 0.02),
        "final_mod_w": nrm((D, 2 * D), 0.5 * D ** -0.5),
        "final_mod_b": nrm((2 * D,), 0.02),
    }


def reference(x, c, positions, mod_w, mod_b, norm1, w_in, mla_q_norm, mla_w_uq, mla_kv_norm,
              mla_w_ukv, ret_gn, lru_conv_w, lru_conv_b, lru_w_a, lru_b_a, lru_w_i, lru_b_i,
              lru_lambda, w_out, norm2, w_gate_up, w_down, final_norm, final_mod_w, final_mod_b):
    B = x.shape[0]
    c_act = jax.nn.silu(c)
    for l in range(DEPTH):
        mod = (c_act @ mod_w[l] + mod_b[l]).reshape(B, N_MOD, D_MODEL)[:, :, None, :]
        sh1, sc1, g1, sh2, sc2, g2 = [mod[:, j] for j in range(N_MOD)]
        h = rms_norm(x, norm1[l]) * (1.0 + sc1) + sh1
        c_q, c_kv, k_rope, r_q, r_k, r_v, r_g, u_x, u_g = split_cols(h @ w_in[l])
        y_a = mla(c_q, c_kv, k_rope, positions, mla_q_norm[l], mla_w_uq[l],
                  mla_kv_norm[l], mla_w_ukv[l])
        y_b = retention(r_q, r_k, r_v, r_g, positions, ret_gn[l])
        y_c = rg_lru_block(u_x, u_g, lru_conv_w[l], lru_conv_b[l], lru_w_a[l], lru_b_a[l],
                           lru_w_i[l], lru_b_i[l], lru_lambda[l])
        y = jnp.concatenate([y_a, y_b, y_c], axis=-1) @ w_out[l]
        x = x + g1 * y
        h = rms_norm(x, norm2[l]) * (1.0 + sc2) + sh2
        gate, up = jnp.split(h @ w_gate_up[l], 2, axis=-1)
        x = x + g2 * ((jax.nn.silu(gate) * up) @ w_down[l])
    f_shift, f_scale = jnp.split((c_act @ final_mod_w + final_mod_b)[:, None, :], 2, axis=-1)
    return rms_norm(x, final_norm) * (1.0 + f_scale) + f_shift
```

```python
import math
import numpy as np
import concourse.bass as bass
import concourse.mybir as mybir
from concourse.bass_utils import run_bass_kernel_spmd

F32 = mybir.dt.float32
BF16 = mybir.dt.bfloat16
I32 = mybir.dt.int32
AF = mybir.ActivationFunctionType
ALU = mybir.AluOpType
AX = mybir.AxisListType


class Buf:
    __slots__ = ("name", "last_w", "readers", "dma_sem", "dma_cnt", "excl")

    def __init__(self, name, excl=False):
        self.name = name
        self.excl = excl
        self.last_w = None
        self.readers = []
        self.dma_sem = None
        self.dma_cnt = 0


class Op:
    __slots__ = ("eng", "fn", "deps", "signal", "val", "sem", "is_dma", "idx")

    def __init__(self, eng, fn, is_dma=False):
        self.eng = eng
        self.fn = fn
        self.deps = []
        self.signal = False
        self.val = None
        self.sem = None
        self.is_dma = is_dma
        self.idx = -1


ENGS = ("pe", "act", "dve", "pool", "sp")


class Prog:
    def __init__(self, nc, stack):
        self.nc = nc
        self.stack = stack
        self.ops = {e: [] for e in ENGS}
        self.all_ops = []
        self.n_dma_sems = 0
        self.last_dma = {}

    def buf(self, name):
        return Buf(name)

    def _add(self, op, reads, writes):
        deps = []
        for b in reads:
            w = b.last_w
            if w is not None:
                deps.append(w)
            if b.excl:
                for r in b.readers:
                    if r.eng != op.eng:
                        deps.append(r)
            b.readers.append(op)
        for b in writes:
            w = b.last_w
            if w is not None and (w.eng != op.eng or op.is_dma or w.is_dma or op.eng != "pe"):
                if not (w.is_dma and op.is_dma and w.sem is op.sem):
                    deps.append(w)
            for r in b.readers:
                if r is not op and (r.eng != op.eng or op.is_dma or r.is_dma or op.eng != "pe"):
                    deps.append(r)
            b.last_w = op
            b.readers = []
        seen = set()
        for d in deps:
            if id(d) not in seen and d is not op:
                seen.add(id(d))
                op.deps.append(d)
                d.signal = True
        op.idx = len(self.all_ops)
        self.all_ops.append(op)
        self.ops[op.eng].append(op)
        return op

    def op(self, eng, fn, reads=(), writes=()):
        return self._add(Op(eng, fn), reads, writes)

    def dma(self, fn, reads=(), writes=(), sem_buf=None, eng="sp"):
        op = Op(eng, fn, is_dma=True)
        op.sem = sem_buf
        op.signal = True
        self.last_dma[id(sem_buf)] = op
        return self._add(op, reads, writes)

    def barrier(self):
        lasts = [self.ops[e][-1] for e in ENGS if self.ops[e]] + list(self.last_dma.values())
        for e in ENGS:
            op = Op(e, None)
            for l in lasts:
                if l.eng != e or l.is_dma:
                    op.deps.append(l)
                    l.signal = True
            op.idx = len(self.all_ops)
            self.all_ops.append(op)
            self.ops[e].append(op)

    def wait_all(self, eng, ops):
        op = Op(eng, None)
        for o in ops:
            op.deps.append(o)
            o.signal = True
        op.idx = len(self.all_ops)
        self.all_ops.append(op)
        self.ops[eng].append(op)

    def emit(self):
        nc = self.nc
        esem = {e: self.stack.enter_context(nc.semaphore("c_" + e)) for e in ENGS}
        for op in self.all_ops:
            if op.is_dma:
                b = op.sem
                if b.dma_sem is None:
                    b.dma_sem = self.stack.enter_context(nc.semaphore("d%d" % self.n_dma_sems))
                    self.n_dma_sems += 1
                b.dma_cnt += 16
                op.val = b.dma_cnt
                op.sem = b.dma_sem
        for e in ENGS:
            cnt = 0
            for op in self.ops[e]:
                if op.is_dma:
                    continue
                if op.signal and op.fn is not None:
                    cnt += 1
                    op.val = cnt
                    op.sem = esem[e]
        self.stats = {e: [0, 0] for e in ENGS}

        def resolve(d, out):
            if d.fn is None:
                for dd in d.deps:
                    resolve(dd, out)
            else:
                out.append(d)

        def stream(e, engobj):
            waited = {}
            for op in self.ops[e]:
                need = {}
                flat = []
                for d in op.deps:
                    resolve(d, flat)
                for d in flat:
                    k = id(d.sem)
                    if waited.get(k, 0) >= d.val:
                        continue
                    if k not in need or need[k][1] < d.val:
                        need[k] = (d.sem, d.val)
                for k, (s, v) in need.items():
                    engobj.wait_ge(s, v)
                    waited[k] = v
                    self.stats[e][1] += 1
                if op.fn is None:
                    continue
                inst = op.fn(engobj)
                self.stats[e][0] += 1
                if op.is_dma:
                    inst.then_inc(op.sem, 16)
                elif op.signal:
                    inst.then_inc(op.sem, 1)

        with nc.Block() as block:
            @block.sync
            def _(e):
                stream("sp", e)

            @block.tensor
            def _(e):
                stream("pe", e)

            @block.scalar
            def _(e):
                stream("act", e)

            @block.vector
            def _(e):
                stream("dve", e)

            @block.gpsimd
            def _(e):
                stream("pool", e)


D = 1024
DC = 8
HM, NOPE, ROPE, VD = 6, 64, 32, 64
HR, DK, DV = 4, 32, 64
LW = 384
FF = 2816
NJ = 22
D_IN = 1952
EPS = 1e-6
NPIECE = 58
SLOT = 2048
NV = 91
V_N1, V_N2, V_QN, V_KVN, V_CW, V_CB, V_BA, V_BI, V_LAM, V_MB = 0, 8, 16, 18, 19, 31, 34, 37, 40, 43
C_ID, C_TRI, C_DEC, C_QD, C_KD, C_INVF, C_CD, NCONST = 0, 128, 256, 768, 896, 1024, 1025, 1026


class Cfg:
    def __init__(self, S=2048, NSEQ=2, L=4, TT=512):
        self.S, self.NSEQ, self.L, self.TT = S, NSEQ, L, TT


def make_consts():
    c = np.zeros((128, NCONST), np.float32)
    c[:, C_ID:C_ID + 128] = np.eye(128, dtype=np.float32)
    k = np.arange(128)[:, None]
    q = np.arange(128)[None, :]
    c[:, C_TRI:C_TRI + 128] = (q >= k).astype(np.float32)
    log_g = np.log(1.0 - np.exp2(-5.0 - np.arange(HR, dtype=np.float32))).astype(np.float32)
    sc = np.float32(DK ** -0.5)
    for h in range(HR):
        diff = (q - k).astype(np.float32)
        dec = np.where(diff >= 0, np.exp(log_g[h] * np.maximum(diff, 0.0)), 0.0).astype(np.float32)
        c[:, C_DEC + h * 128:C_DEC + (h + 1) * 128] = dec * sc
        c[32 * h:32 * h + 32, C_QD:C_QD + 128] = np.exp(log_g[h] * (np.arange(128, dtype=np.float32) + 1.0))[None, :]
        c[:, C_KD + 32 * h:C_KD + 32 * h + 32] = (np.exp(log_g[h] * (127.0 - np.arange(128, dtype=np.float32))) * sc)[:, None]
        c[32 * h:32 * h + 32, C_CD] = np.exp(log_g[h] * 128.0)
    inv = (10000.0 ** (-np.arange(16, dtype=np.float32) / 16.0)).astype(np.float32)
    c[:, C_INVF] = inv[np.arange(128) % 16]
    return c


def build_program(cfg):
    from contextlib import ExitStack
    S, NSEQ, L, TT = cfg.S, cfg.NSEQ, cfg.L, cfg.TT
    NT, NB, BPT = S // TT, S // 128, TT // 128
    NSLOT = 6
    nc = bass.Bass("TRN2", target_bir_lowering=False)
    stack = ExitStack()
    P = Prog(nc, stack)

    def din(name, shape, dt=F32):
        return nc.dram_tensor(name, shape, dt, kind="ExternalInput").ap()

    x_d = din("x", [NSEQ, S, D])
    cT_d = din("cT", [128, DC, NSEQ])
    pos_d = din("pos", [NSEQ, S], I32)
    modw_d = din("mod_w", [L, D, 6 * D])
    win_d = din("w_in", [L, D, D_IN])
    wuq_d = din("w_uq", [L, 256, 576])
    wukv_d = din("w_ukv", [L, 128, 768])
    wa_d = din("lru_w_a", [L, 6, 64, 64])
    wi_d = din("lru_w_i", [L, 6, 64, 64])
    wout_d = din("w_out", [L, D, D])
    wgu_d = din("w_gu", [L, D, 2 * FF])
    wdn_d = din("w_down", [L, FF, D])
    fmodw_d = din("fmod_w", [D, 2 * D])
    vecs_d = din("vecs", [128, L, NV])
    fvec_d = din("fvec", [128, 24])
    gn_d = din("ret_gn", [L, 256])
    consts_d = din("consts", [128, NCONST])
    out_d = nc.dram_tensor("out", [NSEQ, S, D], F32, kind="ExternalOutput").ap()
    wscr = nc.dram_tensor("wscr", [L, NPIECE, 128, SLOT], BF16).ap()
    cs_scr = nc.dram_tensor("cs_scr", [NSEQ, 2, 128, S], F32).ap()

    def sb(name, shape, dt=F32):
        return stack.enter_context(nc.sbuf_tensor("s_" + name, shape, dt))

    SK = max(S, 2048)
    SX = max(S, 2048)
    xT = sb("xT", [128, DC, SX])
    kT = sb("kT", [128, HM, SK], BF16)
    Vc = sb("Vc", [128, NB, HM * 65], BF16)
    hT = sb("hT", [128, DC, TT], BF16)
    arena = sb("arena", [128, NJ * 512], BF16)
    arena_f = arena[:].bitcast(F32)
    mixA = sb("mixA", [64, HM, TT], BF16)
    mixB = sb("mixB", [128, 2, TT], BF16)
    mixC = sb("mixC", [128, 3, TT], BF16)
    ring = sb("ring", [128, NSLOT, SLOT], BF16)
    consts = sb("consts", [128, NCONST])
    identb = sb("identb", [128, 128], BF16)
    trib = sb("trib", [128, 128], BF16)
    onesb = sb("onesb", [128, 128], BF16)
    vecs = sb("vecs", [128, L, NV])
    fvec = sb("fvec", [128, 24])
    gnbc = sb("gnbc", [128, 256])
    cact = sb("cact", [128, DC, NSEQ])
    modraw = sb("modraw", [128, L * 48 + 16, NSEQ])
    modA = sb("modA", [128, L * 2 + 1, DC, NSEQ])
    cneg = sb("cneg", [128, L, 6])
    cossin = sb("cossin", [128, 2, TT])
    cqn = sb("cqn", [128, 2, TT], BF16)
    ckvn = sb("ckvn", [128, TT], BF16)
    qTall = sb("qTall", [128, 2, TT], BF16)
    rqd = sb("rqd", [128, TT], BF16)
    rkT = sb("rkT", [128, TT], BF16)
    qbd = sb("qbd", [128, BPT, HR, 128], BF16)
    uxb = sb("uxb", [128, 3, TT + 3])
    uhist = sb("uhist", [128, 3, 3])
    gel = sb("gel", [128, 3, TT], BF16)
    rstd = [sb("rstd%d" % i, [128, TT]) for i in range(2)]
    stf = sb("stf", [128, 256])
    stb = sb("stb", [128, 256], BF16)
    carry = sb("carry", [128, 3])
    small = sb("small", [128, 64])
    small2 = [sb("small2_%d" % i, [128, 32]) for i in range(2)]
    sgs = sb("sgs", [128, 3, 256], BF16)
    b_sgs = [Buf("sg%d" % i) for i in range(3)]
    banks = [stack.enter_context(nc.psum_tensor("ps%d" % i, [128, 512], F32)) for i in range(8)]

    B = P.buf
    b_xT = [[B("x%d_%d" % (c, t)) for t in range(NT)] for c in range(DC)]
    b_kT = [[B("k%d_%d" % (h, t)) for t in range(NT)] for h in range(HM)]
    b_V = [B("V%d" % t) for t in range(NT)]
    b_hT = [B("h%d" % c) for c in range(DC)]
    b_mixA = [B("mA%d" % h) for h in range(HM)]
    b_mixB = [B("mB%d" % i) for i in range(2)]
    b_mixC = [B("mC%d" % i) for i in range(3)]
    b_ring = [B("ring%d" % i) for i in range(NSLOT)]
    b_bank = [Buf("bank%d" % i, excl=True) for i in range(8)]
    b_const, b_identb, b_trib, b_onesb = B("consts"), B("identb"), B("trib"), B("onesb")
    b_vecs, b_fvec, b_gn, b_cact, b_modraw, b_modA, b_cneg = B("vecs"), B("fvec"), B("gn"), B("cact"), B("modraw"), B("modA"), B("cneg")
    b_cs = B("cossin")
    b_cqn, b_ckvn = [B("cqn0"), B("cqn1")], B("ckvn")
    b_qT = [B("qT%d" % h) for h in range(2)]
    b_rqd, b_rkT, b_qbd = B("rqd"), B("rkT"), B("qbd")
    b_ux = [B("ux%d" % c) for c in range(3)]
    b_uh = [B("uh%d" % c) for c in range(3)]
    b_gel = [B("gel%d" % c) for c in range(3)]
    b_rstd = [B("rstd0"), B("rstd1")]
    b_small2 = [B("small2_0"), B("small2_1")]
    b_stfh = [B("stf_h%d" % h) for h in range(HR)]
    b_stf, b_stb, b_carry, b_small = B("stf"), B("stb"), [B("carry%d" % c) for c in range(3)], B("small")
    b_wscr = [[B("wscr%d_%d" % (l, p)) for p in range(NPIECE)] for l in range(L)]
    b_csscr = [B("csscr%d" % b) for b in range(NSEQ)]

    class Pool:
        def __init__(self, aps, name):
            self.aps = aps
            self.bufs = [B("%s%d" % (name, i)) for i in range(len(aps))]
            self.i = 0

        def get(self):
            k = self.i % len(self.aps)
            self.i += 1
            return self.aps[k], self.bufs[k]

    NF = 6
    poolF = Pool([arena_f[:, i * 512:(i + 1) * 512] for i in range(NF)], "F")
    poolH = Pool([arena[:, NF * 1024 + i * 512: NF * 1024 + (i + 1) * 512] for i in range(6)], "H")
    rden_ap = arena_f[:, 4608:5120]
    bcs_ap = arena_f[:, 5120:5632]
    b_rden, b_bcs = B("rden"), B("bcs")
    b_rden2 = B("rden2")
    rdens, b_rdens = [rden_ap, bcs_ap], [b_rden, b_rden2]
    xcb_t = sb("xcb_t", [128, TT], BF16)
    b_xcbt = B("xcb_t")
    b_act = [poolF.bufs[j // 2] for j in range(12)] + [poolH.bufs[j] for j in range(6)] + [b_rden, b_rden, b_bcs, b_bcs]
    kT_f = kT[:].rearrange("p h s -> p (h s)").bitcast(F32)
    xT_flat = xT[:].rearrange("p c s -> p (c s)")
    stgs = Pool([kT_f[:, i * 2048:(i + 1) * 2048] for i in range(3)], "stg")
    blks = [xT_flat[:, 0:8192], xT_flat[:, 8192:16384]]
    b_blk = [B("blkA"), B("blkB")]
    cvo = Pool([arena[:, i * 2048:(i + 1) * 2048] for i in range(5)], "cvo")

    class PS:
        free = list(range(8))

        @staticmethod
        def get():
            i = PS.free.pop(0)
            return i

        @staticmethod
        def put(i):
            PS.free.append(i)

    def mm(out, lhsT, rhs, start, stop, reads, writes, sgc=False):
        return P.op("pe", lambda e: e.matmul(out, lhsT=lhsT, rhs=rhs, start=start, stop=stop, skip_group_check=sgc), reads, writes)

    ldq = {"alt": False, "n": 0}

    def ld(out_ap, in_ap, reads, writes, sem_buf):
        eng = "sp"
        if ldq["alt"]:
            ldq["n"] += 1
            eng = "act" if ldq["n"] % 2 else "sp"
        return P.dma(lambda e: e.dma_start(out=out_ap, in_=in_ap), reads=reads, writes=writes, sem_buf=sem_buf, eng=eng)

    def act(out, in_, func, reads, writes, scale=1.0, bias=0.0):
        return P.op("act", lambda e: e.activation(out=out, in_=in_, func=func, scale=scale, bias=bias), reads, writes)

    def tt(eng, out, in0, in1, op, reads, writes):
        return P.op(eng, lambda e: e.tensor_tensor(out=out, in0=in0, in1=in1, op=op), reads, writes)

    def ts(eng, out, in0, s1, s2, op0, op1, reads, writes):
        return P.op(eng, lambda e: e.tensor_scalar(out=out, in0=in0, scalar1=s1, scalar2=s2, op0=op0, op1=op1), reads, writes)

    def stt(out, in0, scalar, in1, op0, op1, reads, writes):
        return P.op("dve", lambda e: e.scalar_tensor_tensor(out=out, in0=in0, scalar=scalar, in1=in1, op0=op0, op1=op1), reads, writes)

    def cp(eng, out, in_, reads, writes):
        if eng == "act":
            return P.op("act", lambda e: e.copy(out=out, in_=in_), reads, writes)
        return P.op(eng, lambda e: e.tensor_copy(out=out, in_=in_), reads, writes)

    ld(consts[:], consts_d, [], [b_const], b_const)
    ld(vecs[:], vecs_d, [], [b_vecs], b_vecs)
    ld(fvec[:], fvec_d, [], [b_fvec], b_fvec)
    ld(cact[:], cT_d, [], [b_cact], b_cact)
    cp("dve", identb[:], consts[:, C_ID:C_ID + 128], [b_const], [b_identb])
    cp("dve", trib[:], consts[:, C_TRI:C_TRI + 128], [b_const], [b_trib])
    P.op("pool", lambda e: e.memset(onesb[:], 1.0), [], [b_onesb])
    ones_f = sb("ones_f", [128, 64])
    b_onesf = B("ones_f")
    P.op("pool", lambda e: e.memset(ones_f[:], 1.0), [], [b_onesf])
    P.op("pool", lambda e: e.memset(Vc[:], 1.0), [], b_V)
    P.op("pool", lambda e: e.memset(qbd[:], 0.0), [], [b_qbd])
    act(cact[:], cact[:], AF.Silu, [b_cact], [b_cact])
    for l in range(L):
        act(small[:, 0:3], vecs[:, l, V_LAM:V_LAM + 3], AF.Exp, [b_vecs], [b_small], scale=-1.0)
        act(small[:, 3:6], small[:, 0:3], AF.Ln, [b_small], [b_small], bias=1.0)
        ts("dve", cneg[:, l, 0:3], small[:, 3:6], -8.0, None, ALU.mult, ALU.bypass, [b_small], [b_cneg])
        ts("dve", cneg[:, l, 3:6], small[:, 3:6], -16.0, None, ALU.mult, ALU.bypass, [b_small], [b_cneg])

    cast_rr = [0]

    def cast(dst, src, bs, bd, scale=None):
        eng = ("dve", "act", "dve", "act", "dve", "dve", "act", "dve", "pool", "dve", "act", "dve")[cast_rr[0] % 12]
        cast_rr[0] += 1
        if scale is None:
            cp(eng, dst, src, [bs], [bd])
        elif eng == "act":
            P.op("act", lambda e: e.mul(out=dst, in_=src, mul=scale), [bs], [bd])
        else:
            ts(eng, dst, src, scale, None, ALU.mult, ALU.bypass, [bs], [bd])

    cact_b = sb("cact_b", [128, DC, NSEQ], BF16)
    b_cactb = B("cact_b")
    cp("dve", cact_b[:], cact[:], [b_cact], [b_cactb])
    def mod_group(src_ap, col0, bias_ap_fn):
        pb = PS.get()
        for kc in range(DC):
            stg, bstg = stgs.get()
            ld(stg, src_ap[kc * 128:(kc + 1) * 128, :], [], [bstg], bstg)
            wb, bwb = cvo.get()
            cast(wb, stg, bstg, bwb)
            for cc in range(16):
                mm(banks[pb][:, cc * NSEQ:(cc + 1) * NSEQ], wb[:, cc * 128:(cc + 1) * 128], cact_b[:, kc, :],
                   kc == 0 and cc == 0, kc == DC - 1 and cc == 15, [bwb, b_cactb], [b_bank[pb]], sgc=True)
        for cc in range(16):
            act(modraw[:, col0 + cc, :], banks[pb][:, cc * NSEQ:(cc + 1) * NSEQ], AF.Identity, [b_bank[pb], b_vecs, b_fvec], [b_modraw],
                bias=bias_ap_fn(cc))
        PS.put(pb)

    ldq["alt"] = True
    for l in range(L):
        for g in range(3):
            mod_group(modw_d[l][:, g * 2048:(g + 1) * 2048], l * 48 + g * 16,
                      lambda cc, l=l, g=g: vecs[:, l, V_MB + g * 16 + cc: V_MB + g * 16 + cc + 1])
    mod_group(fmodw_d[:, 0:2048], L * 48, lambda cc: fvec[:, 8 + cc: 8 + cc + 1])
    for bq in range(NSEQ):
        for l in range(L):
            stt(modA[:, l * 2 + 0, :, bq], modraw[:, l * 48 + 8: l * 48 + 16, bq], 1.0, vecs[:, l, V_N1:V_N1 + 8], ALU.add, ALU.mult,
                [b_modraw, b_vecs], [b_modA])
            stt(modA[:, l * 2 + 1, :, bq], modraw[:, l * 48 + 32: l * 48 + 40, bq], 1.0, vecs[:, l, V_N2:V_N2 + 8], ALU.add, ALU.mult,
                [b_modraw, b_vecs], [b_modA])
        stt(modA[:, L * 2, :, bq], modraw[:, L * 48 + 8: L * 48 + 16, bq], 1.0, fvec[:, 0:8], ALU.add, ALU.mult,
            [b_modraw, b_fvec], [b_modA])

    def modcol(l, j, c, bq):
        return modraw[:, l * 48 + j * 8 + c, bq:bq + 1]

    def sw(dst_of, src_of, bs, bd):
        cast(dst_of(0, 16), src_of(16, 32), bs, bd, scale=-1.0)
        cast(dst_of(16, 32), src_of(0, 16), bs, bd)

    def v3(ap):
        return ap.rearrange("p (k n) -> p k n", n=256)

    def convert_piece(l, p):
        stg, bstg = stgs.get()
        slot, bsl = cvo.get()
        win = win_d[l].rearrange("(k p) n -> p k n", p=128)
        s3, o3 = v3(stg), v3(slot)
        zero = False

        def L_(dst, src):
            ld(dst, src, [], [bstg], bstg)

        def Z():
            P.op("pool", lambda e: e.memset(slot, 0.0), [], [bsl])

        if p == 0:
            L_(s3, win[:, :, 0:256]); cast(slot, stg, bstg, bsl)
        elif p == 1:
            L_(s3[:, :, 0:128], win[:, :, 256:384]); L_(s3[:, :, 128:160], win[:, :, 384:416]); Z()
            cast(o3[:, :, 0:128], s3[:, :, 0:128], bstg, bsl); cast(o3[:, :, 192:224], s3[:, :, 128:160], bstg, bsl)
        elif p == 2:
            L_(s3[:, :, 0:32], win[:, :, 384:416]); L_(s3[:, :, 128:256], win[:, :, 416:544]); Z()
            sw(lambda a, b: o3[:, :, 64 + a:64 + b], lambda a, b: s3[:, :, a:b], bstg, bsl); cast(o3[:, :, 128:256], s3[:, :, 128:256], bstg, bsl)
        elif p in (3, 4):
            a0 = 416 if p == 3 else 544
            b0 = 544 if p == 3 else 1184
            L_(s3[:, :, 0:128], win[:, :, a0:a0 + 128]); L_(s3[:, :, 128:256], win[:, :, b0:b0 + 128])
            for h in range(HR):
                sw(lambda a, b, h=h: o3[:, :, 32 * h + a:32 * h + b], lambda a, b, h=h: s3[:, :, 32 * h + a:32 * h + b], bstg, bsl)
            cast(o3[:, :, 128:256], s3[:, :, 128:256], bstg, bsl)
        elif p in (5, 6, 8, 9):
            c0 = {5: 1312, 6: 1568, 8: 672, 9: 928}[p]
            L_(s3, win[:, :, c0:c0 + 256]); cast(slot, stg, bstg, bsl)
        elif p == 7:
            L_(s3[:, :, 0:128], win[:, :, 1824:1952]); Z(); cast(o3[:, :, 0:128], s3[:, :, 0:128], bstg, bsl)
        elif p == 10:
            L_(stg[:, 0:1152].rearrange("p (k n) -> p k n", n=576), wuq_d[l].rearrange("(k p) n -> p k n", p=128))
            cast(slot[:, 0:1152], stg[:, 0:1152], bstg, bsl)
            for kc in range(2):
                for h in range(HM):
                    o = 1152 + (kc * HM + h) * 32
                    i = kc * 576 + h * 96 + 64
                    sw(lambda a, b, o=o: slot[:, o + a:o + b], lambda a, b, i=i: stg[:, i + a:i + b], bstg, bsl)
        elif p == 11:
            L_(stg[:, 0:768], wukv_d[l])
            for g in range(6):
                r0, cc = (g % 2) * 64, g // 2
                L_(stg[r0:r0 + 64, 768 + cc * 128 + r0: 768 + cc * 128 + r0 + 64], wa_d[l, g])
                L_(stg[r0:r0 + 64, 1152 + cc * 128 + r0: 1152 + cc * 128 + r0 + 64], wi_d[l, g])
            Z()
            kv3 = stg[:, 0:768].rearrange("p (h n) -> p h n", n=128)
            cast(slot[:, 0:384].rearrange("p (h n) -> p h n", n=64), kv3[:, :, 0:64], bstg, bsl)
            cast(slot[:, 384:768].rearrange("p (h n) -> p h n", n=64), kv3[:, :, 64:128], bstg, bsl)
            for g in range(6):
                r0, cc = (g % 2) * 64, g // 2
                for base in (768, 1152):
                    o = base + cc * 128 + r0
                    cast(slot[r0:r0 + 64, o:o + 64], stg[r0:r0 + 64, o:o + 64], bstg, bsl)
        elif 12 <= p < 20:
            c = p - 12
            L_(stg[0:64, 0:768].rearrange("p (h n) -> p h n", n=128),
               wout_d[l][0:384, c * 128:(c + 1) * 128].rearrange("(h r) n -> r h n", r=64))
            L_(stg[:, 768:1408].rearrange("p (k n) -> p k n", n=128),
               wout_d[l][384:1024, c * 128:(c + 1) * 128].rearrange("(k p) n -> p k n", p=128))
            Z()
            cast(slot[0:64, 0:768], stg[0:64, 0:768], bstg, bsl); cast(slot[:, 768:1408], stg[:, 768:1408], bstg, bsl)
        elif 20 <= p < 42:
            jj, isup = (p - 20) // 2, (p - 20) % 2
            wgu = wgu_d[l].rearrange("(k p) n -> p k n", p=128)
            c0 = isup * FF + jj * 256
            L_(s3, wgu[:, :, c0:c0 + 256])
            cast(slot, stg, bstg, bsl)
        else:
            c, hf = (p - 42) // 2, (p - 42) % 2
            L_(stg[:, 0:1408].rearrange("p (j n) -> p j n", n=128),
               wdn_d[l][hf * 1408:(hf + 1) * 1408, c * 128:(c + 1) * 128].rearrange("(j p) n -> p j n", p=128))
            cast(slot[:, 0:1408], stg[:, 0:1408], bstg, bsl)
        P.dma(lambda e: e.dma_start(out=wscr[l, p], in_=slot), reads=[bsl], writes=[b_wscr[l][p]], sem_buf=bsl)

    def store_piece(l, p, slot, bsl):
        P.dma(lambda e: e.dma_start(out=wscr[l, p], in_=slot), reads=[bsl], writes=[b_wscr[l][p]], sem_buf=bsl)

    def conv_win(l):
        W0 = blks[0][:, 0:8 * 928].rearrange("p (k n) -> p k n", n=928)
        W1 = blks[1][:, 0:8192].rearrange("p (k n) -> p k n", n=1024)
        for kc in range(DC):
            ld(W0[:, kc, :], win_d[l][kc * 128:(kc + 1) * 128, 0:928], [], [b_blk[0]], b_blk[0])
            ld(W1[:, kc, :], win_d[l][kc * 128:(kc + 1) * 128, 928:1952], [], [b_blk[1]], b_blk[1])

        def src(a, b):
            return (W0[:, :, a:b], b_blk[0]) if b <= 928 else (W1[:, :, a - 928:b - 928], b_blk[1])

        def plain(o3, d0, a, b, bsl):
            sv, bs = src(a, b)
            cast(o3[:, :, d0:d0 + (b - a)], sv, bs, bsl)

        def swp(o3, d0, a, bsl):
            sv, bs = src(a, a + 32)
            cast(o3[:, :, d0:d0 + 16], sv[:, :, 16:32], bs, bsl, scale=-1.0)
            cast(o3[:, :, d0 + 16:d0 + 32], sv[:, :, 0:16], bs, bsl)

        for p in range(10):
            slot, bsl = cvo.get()
            o3 = v3(slot)
            if p in (1, 2, 7):
                P.op("pool", lambda e, slot=slot: e.memset(slot, 0.0), [], [bsl])
            if p == 0:
                plain(o3, 0, 0, 256, bsl)
            elif p == 1:
                plain(o3, 0, 256, 384, bsl); plain(o3, 192, 384, 416, bsl)
            elif p == 2:
                swp(o3, 64, 384, bsl); plain(o3, 128, 416, 544, bsl)
            elif p in (3, 4):
                a0 = 416 if p == 3 else 544
                b0 = 544 if p == 3 else 1184
                for h in range(HR):
                    swp(o3, 32 * h, a0 + 32 * h, bsl)
                plain(o3, 128, b0, b0 + 128, bsl)
            elif p in (5, 6, 8, 9):
                c0 = {5: 1312, 6: 1568, 8: 672, 9: 928}[p]
                plain(o3, 0, c0, c0 + 256, bsl)
            elif p == 7:
                plain(o3, 0, 1824, 1952, bsl)
            store_piece(l, p, slot, bsl)

    def conv_gu(l):
        for blk in range(6):
            c0 = blk * 1024
            n = min(1024, 2 * FF - c0)
            st, bst = blks[blk % 2], b_blk[blk % 2]
            v = st[:, 0:8 * n].rearrange("p (k n) -> p k n", n=n)
            for kc in range(DC):
                ld(v[:, kc, :], wgu_d[l][kc * 128:(kc + 1) * 128, c0:c0 + n], [], [bst], bst)
            for off in range(0, n, 256):
                col = c0 + off
                isup = 1 if col >= FF else 0
                jj = (col - isup * FF) // 256
                slot, bsl = cvo.get()
                cast(v3(slot), v[:, :, off:off + 256], bst, bsl)
                store_piece(l, 20 + 2 * jj + isup, slot, bsl)

    def conv_d(l):
        for hf in range(2):
            for cg in range(2):
                k = hf * 2 + cg
                st, bst = blks[k % 2], b_blk[k % 2]
                v = st[:, 0:11 * 512].rearrange("p (j n) -> p j n", n=512)
                for j in range(11):
                    r0 = (hf * 11 + j) * 128
                    ld(v[:, j, :], wdn_d[l][r0:r0 + 128, cg * 512:(cg + 1) * 512], [], [bst], bst)
                for cc in range(4):
                    c = cg * 4 + cc
                    slot, bsl = cvo.get()
                    cast(slot[:, 0:1408].rearrange("p (j n) -> p j n", n=128), v[:, :, cc * 128:(cc + 1) * 128], bst, bsl)
                    store_piece(l, 42 + 2 * c + hf, slot, bsl)

    for l in range(L):
        conv_win(l)
        for p in range(10, 20):
            convert_piece(l, p)
        conv_gu(l)
        conv_d(l)
    ldq["alt"] = False
    P.barrier()

    seq_list = [(l, p) for _b in range(NSEQ) for l in range(L) for _t in range(NT) for p in range(NPIECE)]
    wstate = {"issued": 0, "cur": 0}

    def wget():
        k = wstate["cur"]
        wstate["cur"] += 1
        lim = min(len(seq_list), k + NSLOT - 1)
        while wstate["issued"] < lim:
            i = wstate["issued"]
            l_, p_ = seq_list[i]
            s_ = i % NSLOT
            ld(ring[:, s_, :], wscr[l_, p_], [b_wscr[l_][p_]], [b_ring[s_]], b_ring[s_])
            wstate["issued"] += 1
        return ring[:, k % NSLOT, :], b_ring[k % NSLOT]

    hT_f = hT[:].rearrange("p c t -> p (c t)").bitcast(F32)
    xstg = Pool([hT_f[:, i * 1024:(i + 1) * 1024] for i in range(2)], "xs")
    ident_f = consts[:, C_ID:C_ID + 128]
    ffn_tmp = Pool([rstd[0][:], rstd[1][:]], "ft")
    ffn_tmp.bufs[0] = b_rstd[0]
    ffn_tmp.bufs[1] = b_rstd[1]
    log_g = np.log(1.0 - np.exp2(-5.0 - np.arange(HR, dtype=np.float32))).astype(np.float32)
    CDEC = [float(np.exp(log_g[h] * 128.0)) for h in range(HR)]
    TWO_PI = 2.0 * math.pi
    out_stores = []

    def sumsq_rstd(srcs, nfeat, r, br):
        pb = PS.get()
        for i, (ap, bl) in enumerate(srcs):
            sq, bsq = poolH.get()
            act(sq, ap, AF.Square, bl, [bsq])
            mm(banks[pb][:, 0:TT], onesb[:], sq, i == 0, i == len(srcs) - 1, [b_onesb, bsq], [b_bank[pb]])
        act(r, banks[pb][:, 0:TT], AF.Ln, [b_bank[pb]], [br], scale=1.0 / nfeat, bias=EPS)
        act(r, r, AF.Exp, [br], [br], scale=-0.5)
        PS.put(pb)

    def rms_mod(t, scale_of, bias_of, out_of, out_bufs):
        sl = slice(t * TT, (t + 1) * TT)
        r, br = rstd[0][:], b_rstd[0]
        sumsq_rstd([(xT[:, c, sl], [b_xT[c][t]]) for c in range(DC)], D, r, br)
        for c in range(DC):
            tmp, btmp = poolF.get()
            tt("dve", tmp, xT[:, c, sl], r, ALU.mult, [b_xT[c][t], br], [btmp])
            act(out_of(c), tmp, AF.Identity, [btmp, b_modA, b_modraw], [out_bufs[c]], scale=scale_of(c), bias=bias_of(c))

    def proj_fm(slot, bsl, col0, M, pb):
        s3 = v3(slot)
        for kc in range(DC):
            mm(banks[pb][0:M, 0:TT], s3[:, kc, col0:col0 + M], hT[:, kc, :], kc == 0, kc == DC - 1, [bsl, b_hT[kc]], [b_bank[pb]])

    def rot(pa, pbk, rows, out_fn):
        t1, bt1 = poolF.get()
        t2, bt2 = poolF.get()
        tt("dve", t1[rows, :], banks[pa][rows, 0:TT], cossin[rows, 0, :], ALU.mult, [b_bank[pa], b_cs], [bt1])
        tt("dve", t2[rows, :], banks[pbk][rows, 0:TT], cossin[rows, 1, :], ALU.mult, [b_bank[pbk], b_cs], [bt2])
        out_fn(t1, t2, [bt1, bt2])

    def tile_body(bq, l, t):
        sl = slice(t * TT, (t + 1) * TT)
        ld(cossin[:, 0, :], cs_scr[bq, 0, :, sl], [b_csscr[bq]], [b_cs], b_cs)
        ld(cossin[:, 1, :], cs_scr[bq, 1, :, sl], [b_csscr[bq]], [b_cs], b_cs)
        rms_mod(t, lambda c: modA[:, l * 2, c, bq:bq + 1], lambda c: modcol(l, 0, c, bq), lambda c: hT[:, c, :], b_hT)
        s0, bs0 = wget()
        pq = [PS.get(), PS.get()]
        for kc in range(DC):
            for i in range(2):
                mm(banks[pq[i]][:, 0:TT], v3(s0)[:, kc, i * 128:(i + 1) * 128], hT[:, kc, :], kc == 0, kc == DC - 1,
                   [bs0, b_hT[kc]], [b_bank[pq[i]]])
        r1, br1 = rstd[1][:], b_rstd[1]
        sumsq_rstd([(banks[pq[i]][:, 0:TT], [b_bank[pq[i]]]) for i in range(2)], 256, r1, br1)
        for i in range(2):
            stt(cqn[:, i, :], banks[pq[i]][:, 0:TT], vecs[:, l, V_QN + i:V_QN + i + 1], r1, ALU.mult, ALU.mult,
                [b_bank[pq[i]], b_vecs, br1], [b_cqn[i]])
            PS.put(pq[i])
        s1, bs1 = wget()
        pkv, pkr = PS.get(), PS.get()
        proj_fm(s1, bs1, 0, 128, pkv)
        proj_fm(s1, bs1, 128, 96, pkr)
        sumsq_rstd([(banks[pkv][:, 0:TT], [b_bank[pkv]])], 128, r1, br1)
        stt(ckvn[:], banks[pkv][:, 0:TT], vecs[:, l, V_KVN:V_KVN + 1], r1, ALU.mult, ALU.mult, [b_bank[pkv], b_vecs, br1], [b_ckvn])
        PS.put(pkv)
        s2, bs2 = wget()
        pks, prq = PS.get(), PS.get()
        proj_fm(s2, bs2, 0, 96, pks)
        proj_fm(s2, bs2, 128, 128, prq)

        def kr_out(t1, t2, bl):
            for h in range(HM):
                tt("pool", kT[64:96, h, sl], t1[64:96, :], t2[64:96, :], ALU.add, bl, [b_kT[h][t]])
        rot(pkr, pks, slice(64, 96), kr_out)
        PS.put(pkr)
        PS.put(pks)
        s3_, bs3 = wget()
        prqs, prk = PS.get(), PS.get()
        proj_fm(s3_, bs3, 0, 128, prqs)
        proj_fm(s3_, bs3, 128, 128, prk)

        def rq_out(t1, t2, bl):
            tt("pool", t1, t1, t2, ALU.add, bl, [bl[0]])
            for n in range(BPT):
                tt("pool", rqd[:, n * 128:(n + 1) * 128], t1[:, n * 128:(n + 1) * 128], consts[:, C_QD:C_QD + 128], ALU.mult,
                   [bl[0], b_const], [b_rqd])
            for h in range(HR):
                cp("act" if h % 2 else "pool", qbd[32 * h:32 * h + 32, :, h, :],
                   t1[32 * h:32 * h + 32, :].rearrange("p (n c) -> p n c", c=128), [bl[0]], [b_qbd])
        rot(prq, prqs, slice(0, 128), rq_out)
        PS.put(prq)
        PS.put(prqs)
        s4, bs4 = wget()
        prks = PS.get()
        pux = [PS.get()]
        proj_fm(s4, bs4, 0, 128, prks)
        proj_fm(s4, bs4, 128, 128, pux[0])
        rot(prk, prks, slice(0, 128), lambda t1, t2, bl: tt("pool", rkT[:], t1, t2, ALU.add, bl, [b_rkT]))
        PS.put(prk)
        PS.put(prks)
        s5, bs5 = wget()
        pux += [PS.get(), PS.get()]
        proj_fm(s5, bs5, 0, 128, pux[1])
        proj_fm(s5, bs5, 128, 128, pux[2])
        for c in range(3):
            cp("act", uxb[:, c, 3:3 + TT], banks[pux[c]][:, 0:TT], [b_bank[pux[c]]], [b_ux[c]])
            cp("pool", uxb[:, c, 0:3], uhist[:, c, :], [b_uh[c]], [b_ux[c]])
            cp("pool", uhist[:, c, :], uxb[:, c, TT:TT + 3], [b_ux[c]], [b_uh[c]])
            PS.put(pux[c])
        s6, bs6 = wget()
        pug = [PS.get(), PS.get()]
        proj_fm(s6, bs6, 0, 128, pug[0])
        proj_fm(s6, bs6, 128, 128, pug[1])
        for c in range(2):
            act(gel[:, c, :], banks[pug[c]][:, 0:TT], AF.Gelu_apprx_tanh, [b_bank[pug[c]]], [b_gel[c]])
            PS.put(pug[c])
        s7, bs7 = wget()
        pg2 = PS.get()
        proj_fm(s7, bs7, 0, 128, pg2)
        act(gel[:, 2, :], banks[pg2][:, 0:TT], AF.Gelu_apprx_tanh, [b_bank[pg2]], [b_gel[2]])
        PS.put(pg2)
        s8, bs8 = wget()
        s9, bs9 = wget()
        def ret_fa(n):
            bs_ = slice(n * 128, (n + 1) * 128)
            pv, pg = PS.get(), PS.get()
            for kc in range(DC):
                mm(banks[pv][:, 0:256], hT[:, kc, bs_], v3(s8)[:, kc, :], kc == 0, kc == DC - 1, [b_hT[kc], bs8], [b_bank[pv]])
            for kc in range(DC):
                mm(banks[pg][:, 0:256], hT[:, kc, bs_], v3(s9)[:, kc, :], kc == 0, kc == DC - 1, [b_hT[kc], bs9], [b_bank[pg]])
            psc = PS.get()
            mm(banks[psc][:, 0:512], rkT[:, bs_], qbd[:, n, :, :].rearrange("p h c -> p (h c)"), True, True, [b_rkT, b_qbd], [b_bank[psc]])
            pkt = PS.get()
            mm(banks[pkt][:, 0:128], rkT[:, bs_], identb[:], True, True, [b_rkT, b_identb], [b_bank[pkt]])
            vsb, bvsb = poolH.get()
            cp("act", vsb[:, 0:256], banks[pv][:, 0:256], [b_bank[pv]], [bvsb])
            PS.put(pv)
            sg, bsg = sgs[:, n % 3, :], b_sgs[n % 3]
            act(sg, banks[pg][:, 0:256], AF.Silu, [b_bank[pg]], [bsg])
            PS.put(pg)
            tt("pool", sg, sg, gnbc[:], ALU.mult, [bsg, b_gn], [bsg])
            ptr, bptr = poolH.get()
            tt("dve", ptr, banks[psc][:, 0:512], consts[:, C_DEC:C_DEC + 512], ALU.mult, [b_bank[psc], b_const], [bptr])
            PS.put(psc)
            kdt, bkdt = poolH.get()
            tt("dve", kdt[:, 0:128], banks[pkt][:, 0:128], consts[:, C_KD:C_KD + 128], ALU.mult, [b_bank[pkt], b_const], [bkdt])
            PS.put(pkt)
            return (n, vsb, bvsb, sg, bsg, ptr, bptr, kdt, bkdt)

        def ret_fb(fa):
            n, vsb, bvsb, sg, bsg, ptr, bptr, kdt, bkdt = fa
            bs_ = slice(n * 128, (n + 1) * 128)
            po = PS.get()
            mm(banks[po][:, 0:256], rqd[:, bs_], stb[:], True, False, [b_rqd, b_stb], [b_bank[po]])
            for h in range(HR):
                mm(banks[po][:, h * 64:(h + 1) * 64], ptr[:, h * 128:(h + 1) * 128], vsb[:, h * 64:(h + 1) * 64], False, h == HR - 1,
                   [bptr, bvsb], [b_bank[po]])
            pk2 = PS.get()
            mm(banks[pk2][:, 0:256], kdt[:, 0:128], vsb[:, 0:256], True, True, [bkdt, bvsb], [b_bank[pk2]])
            for h in range(HR):
                rs, cs = slice(32 * h, 32 * h + 32), slice(64 * h, 64 * h + 64)
                stt(stf[rs, cs], stf[rs, cs], CDEC[h], banks[pk2][rs, cs], ALU.mult, ALU.add, [b_stfh[h], b_bank[pk2]], [b_stfh[h]])
            PS.put(pk2)
            cp("pool", stb[:], stf[:], b_stfh, [b_stb])
            return n, po, sg, bsg

        def ret_tail(st):
            n, po, sg, bsg = st
            bs_ = slice(n * 128, (n + 1) * 128)
            sm, bsm = small2[n % 2], b_small2[n % 2]
            osb, bosb = poolF.get()
            cp("act", osb[:, 0:256], banks[po][:, 0:256], [b_bank[po]], [bosb])
            PS.put(po)
            o3 = osb[:, 0:256].rearrange("p (h d) -> p h d", d=64)
            P.op("dve", lambda e, o3=o3, sm=sm: e.reduce_sum(out=sm[:, 0:4], in_=o3, axis=AX.X), [bosb], [bsm])
            osq, bosq = poolF.get()
            act(osq[:, 0:256], osb[:, 0:256], AF.Square, [bosb], [bosq])
            osq3 = osq[:, 0:256].rearrange("p (h d) -> p h d", d=64)
            P.op("dve", lambda e, osq3=osq3, sm=sm: e.reduce_sum(out=sm[:, 4:8], in_=osq3, axis=AX.X), [bosq], [bsm])
            ts("dve", sm[:, 8:12], sm[:, 0:4], 1.0 / 64, None, ALU.mult, ALU.bypass, [bsm], [bsm])
            tt("dve", sm[:, 12:16], sm[:, 8:12], sm[:, 8:12], ALU.mult, [bsm], [bsm])
            stt(sm[:, 16:20], sm[:, 4:8], 1.0 / 64, sm[:, 12:16], ALU.mult, ALU.subtract, [bsm], [bsm])
            act(sm[:, 20:24], sm[:, 16:20], AF.Ln, [bsm], [bsm], bias=EPS)
            act(sm[:, 20:24], sm[:, 20:24], AF.Exp, [bsm], [bsm], scale=-0.5)
            on = osq
            eph = [B("gn_eph%d" % h) for h in range(HR)]
            for h in range(HR):
                ts("dve", on[:, h * 64:(h + 1) * 64], osb[:, h * 64:(h + 1) * 64], sm[:, 8 + h:9 + h], sm[:, 20 + h:21 + h],
                   ALU.subtract, ALU.mult, [bosb, bsm, bosq], [eph[h]])
            ob, bob = poolH.get()
            tt("pool", ob[:, 0:256], on[:, 0:256], sg, ALU.mult, [bosq, bsg] + eph, [bob])
            pt_ = PS.get()
            for i in range(2):
                mm(banks[pt_][:, i * 128:(i + 1) * 128], ob[:, i * 128:(i + 1) * 128], identb[:], True, True, [bob, b_identb], [b_bank[pt_]])
            for i in range(2):
                cp("act", mixB[:, i, bs_], banks[pt_][:, i * 128:(i + 1) * 128], [b_bank[pt_]], [b_mixB[i]])
            PS.put(pt_)

        fas = {0: ret_fa(0)}
        fbs = {}
        for n in range(BPT):
            fbs[n] = ret_fb(fas.pop(n))
            if n + 1 < BPT:
                fas[n + 1] = ret_fa(n + 1)
            if n >= 1:
                ret_tail(fbs.pop(n - 1))
        ret_tail(fbs.pop(BPT - 1))
        sa, bsa = wget()
        sbw, bsb = wget()
        for h in range(HM):
            pk = PS.get()
            mm(banks[pk][0:64, 0:TT], sbw[:, h * 64:(h + 1) * 64], ckvn[:], True, True, [bsb, b_ckvn], [b_bank[pk]])
            cp("act" if h % 2 else "dve", kT[0:64, h, sl], banks[pk][0:64, 0:TT], [b_bank[pk]], [b_kT[h][t]])
            PS.put(pk)
        for n in range(BPT):
            pvv = PS.get()
            mm(banks[pvv][:, 0:384], ckvn[:, n * 128:(n + 1) * 128], sbw[:, 384:768], True, True, [b_ckvn, bsb], [b_bank[pvv]])
            cp("dve", Vc[:, t * BPT + n, :].rearrange("p (h d) -> p h d", d=65)[:, :, 0:64],
               banks[pvv][:, 0:384].rearrange("p (h d) -> p h d", d=64), [b_bank[pvv]], [b_V[t]])
            PS.put(pvv)
        lru = {}

        def lru_conv(c):
            xc, bxc = poolF.get()
            cw = lambda j, c=c: vecs[:, l, V_CW + c * 4 + j:V_CW + c * 4 + j + 1]
            act(xc, uxb[:, c, 3:3 + TT], AF.Identity, [b_ux[c], b_vecs], [bxc], scale=cw(3), bias=vecs[:, l, V_CB + c:V_CB + c + 1])
            for j in (2, 1, 0):
                stt(xc, uxb[:, c, j:j + TT], cw(j), xc, ALU.mult, ALU.add, [b_ux[c], b_vecs, bxc], [bxc])
            xcb, bxcb = xcb_t[:], b_xcbt
            cp("pool", xcb, xc, [bxc], [bxcb])
            lru[c] = (xc, bxc, xcb, bxcb)

        def lru_gates(c):
            xc, bxc, xcb, bxcb = lru[c]
            pr, pi_ = PS.get(), PS.get()
            mm(banks[pr][:, 0:TT], sbw[:, 768 + c * 128:768 + (c + 1) * 128], xcb, True, True, [bsb, bxcb], [b_bank[pr]])
            mm(banks[pi_][:, 0:TT], sbw[:, 1152 + c * 128:1152 + (c + 1) * 128], xcb, True, True, [bsb, bxcb], [b_bank[pi_]])
            lru[c] = (xc, bxc, pr, pi_)

        def lru_rest(c):
            xc, bxc, pr, pi_ = lru[c]
            rr, brr = poolF.get()
            ii, bii = poolF.get()
            act(rr, banks[pr][:, 0:TT], AF.Sigmoid, [b_bank[pr], b_vecs], [brr], bias=vecs[:, l, V_BA + c:V_BA + c + 1])
            yield
            act(ii, banks[pi_][:, 0:TT], AF.Sigmoid, [b_bank[pi_], b_vecs], [bii], bias=vecs[:, l, V_BI + c:V_BI + c + 1])
            PS.put(pr)
            PS.put(pi_)
            yield
            tt("pool", ii, ii, xc, ALU.mult, [bii, bxc], [bii])
            aa, baa = poolF.get()
            s2_, bs2_ = poolF.get()
            act(aa, rr, AF.Exp, [brr, b_cneg], [baa], scale=cneg[:, l, c:c + 1])
            yield
            act(s2_, rr, AF.Exp, [brr, b_cneg], [bs2_], scale=cneg[:, l, 3 + c:4 + c])
            yield
            ts("dve", s2_, s2_, -1.0, 1.0, ALU.mult, ALU.add, [bs2_], [bs2_])
            yield
            act(s2_, s2_, AF.Ln, [bs2_], [bs2_])
            yield
            act(s2_, s2_, AF.Exp, [bs2_], [bs2_], scale=0.5)
            yield
            tt("pool", ii, ii, s2_, ALU.mult, [bii, bs2_], [bii])
            yield
            P.op("dve", lambda e, rr=rr, aa=aa, ii=ii, c=c: e.tensor_tensor_scan(out=rr, data0=aa, data1=ii, initial=carry[:, c:c + 1],
                                                                                op0=ALU.mult, op1=ALU.add),
                 [baa, bii, b_carry[c], brr], [brr])
            yield
            cp("act", carry[:, c:c + 1], rr[:, TT - 1:TT], [brr], [b_carry[c]])
            tt("pool", mixC[:, c, :], rr, gel[:, c, :], ALU.mult, [brr, b_gel[c]], [b_mixC[c]])

        att_scale = float((NOPE + ROPE) ** -0.5)

        def qproj(h):
            qb = h % 2
            pA, pB = PS.get(), PS.get()
            for kc in range(2):
                mm(banks[pA][0:96, 0:TT], sa[:, kc * 576 + h * 96:kc * 576 + h * 96 + 96], cqn[:, kc, :], kc == 0, kc == 1,
                   [bsa, b_cqn[kc]], [b_bank[pA]])
            for kc in range(2):
                o = 1152 + (kc * HM + h) * 32
                mm(banks[pB][0:96, 0:TT], sa[:, o - 64:o + 32], cqn[:, kc, :], kc == 0, kc == 1, [bsa, b_cqn[kc]], [b_bank[pB]])
            cp("act", qTall[0:64, qb, :], banks[pA][0:64, 0:TT], [b_bank[pA]], [b_qT[qb]])
            rot(pA, pB, slice(64, 96),
                lambda t1, t2, bl, qb=qb: tt("pool", qTall[64:96, qb, :], t1[64:96, :], t2[64:96, :], ALU.add, bl, [b_qT[qb]]))
            PS.put(pA)
            PS.put(pB)

        def kloop(h, filler=None):
            qb = h % 2
            po = PS.get()
            nkb = (t + 1) * BPT

            def qk(kb):
                o = kb - t * BPT
                c0 = 0 if o < 0 else o * 128
                tk = kb // BPT
                ps_ = PS.get()
                mm(banks[ps_][:, c0:TT], kT[0:96, h, kb * 128:(kb + 1) * 128], qTall[0:96, qb, c0:TT], True, True,
                   [b_kT[h][tk], b_qT[qb]], [b_bank[ps_]])
                return ps_, o, c0, tk
            ahead = [qk(kb) for kb in range(min(2, nkb))]
            for kb in range(nkb):
                ps_, o, c0, tk = ahead.pop(0)
                if kb + 2 < nkb:
                    ahead.append(qk(kb + 2))
                pt, bpt = poolH.get()
                act(pt[:, c0:TT], banks[ps_][:, c0:TT], AF.Exp, [b_bank[ps_]], [bpt], scale=att_scale)
                PS.put(ps_)
                if o >= 0:
                    tt("dve", pt[:, c0:c0 + 128], pt[:, c0:c0 + 128], trib[:], ALU.mult, [bpt, b_trib], [bpt])
                mm(banks[po][0:65, c0:TT], Vc[:, kb, h * 65:(h + 1) * 65], pt[:, c0:TT], kb == 0, kb == nkb - 1,
                   [b_V[tk], bpt], [b_bank[po]])
                if filler is not None:
                    next(filler, None)
            if filler is not None:
                for _ in filler:
                    pass
            rd, brd = rdens[h % 2], b_rdens[h % 2]
            P.op("dve", lambda e, po=po, rd=rd: e.reciprocal(out=rd[64:65, :], in_=banks[po][64:65, 0:TT]), [b_bank[po]], [brd])
            return po

        def atail(h, po):
            rd, brd = rdens[h % 2], b_rdens[h % 2]
            pbc = PS.get()
            mm(banks[pbc][0:64, 0:TT], ones_f[64:65, 0:64], rd[64:65, :], True, True, [b_onesf, brd], [b_bank[pbc]])
            cp("act", bcs_ap[0:64, :], banks[pbc][0:64, 0:TT], [b_bank[pbc]], [b_bcs])
            PS.put(pbc)
            tt("dve", mixA[:, h, :], banks[po][0:64, 0:TT], bcs_ap[0:64, :], ALU.mult, [b_bank[po], b_bcs], [b_mixA[h]])
            PS.put(po)

        qproj(0)
        pos_ = {}
        for h in range(HM):
            c = h // 2
            if h % 2 == 0:
                lru_conv(c)
                pos_[h] = kloop(h)
                lru_gates(c)
            else:
                pos_[h] = kloop(h, lru_rest(c))
            if h + 1 < HM:
                qproj(h + 1)
            if h >= 1:
                atail(h - 1, pos_.pop(h - 1))
        atail(HM - 1, pos_.pop(HM - 1))
        for c in range(DC):
            sw_, bsw = wget()
            py = PS.get()
            for h in range(HM):
                mm(banks[py][:, 0:TT], sw_[0:64, h * 128:(h + 1) * 128], mixA[:, h, :], h == 0, False, [bsw, b_mixA[h]], [b_bank[py]])
            for i in range(2):
                mm(banks[py][:, 0:TT], sw_[:, 768 + i * 128:768 + (i + 1) * 128], mixB[:, i, :], False, False, [bsw, b_mixB[i]], [b_bank[py]])
            for i in range(3):
                mm(banks[py][:, 0:TT], sw_[:, 768 + (2 + i) * 128:768 + (3 + i) * 128], mixC[:, i, :], False, i == 2,
                   [bsw, b_mixC[i]], [b_bank[py]])
            stt(xT[:, c, sl], banks[py][:, 0:TT], modcol(l, 2, c, bq), xT[:, c, sl], ALU.mult, ALU.add,
                [b_bank[py], b_modraw, b_xT[c][t]], [b_xT[c][t]])
            PS.put(py)
        rms_mod(t, lambda c: modA[:, l * 2 + 1, c, bq:bq + 1], lambda c: modcol(l, 3, c, bq), lambda c: hT[:, c, :], b_hT)
        for jj in range(NJ // 2):
            sg_, bsg_ = wget()
            su_, bsu_ = wget()
            for sub in range(2):
                j = jj * 2 + sub
                pg, pu = PS.get(), PS.get()
                for kc in range(DC):
                    mm(banks[pg][:, 0:TT], v3(sg_)[:, kc, sub * 128:(sub + 1) * 128], hT[:, kc, :], kc == 0, kc == DC - 1,
                       [bsg_, b_hT[kc]], [b_bank[pg]])
                    mm(banks[pu][:, 0:TT], v3(su_)[:, kc, sub * 128:(sub + 1) * 128], hT[:, kc, :], kc == 0, kc == DC - 1,
                       [bsu_, b_hT[kc]], [b_bank[pu]])
                ft, bft = ffn_tmp.get()
                act(ft, banks[pg][:, 0:TT], AF.Silu, [b_bank[pg]], [bft])
                PS.put(pg)
                tt("dve", arena[:, j * TT:(j + 1) * TT], banks[pu][:, 0:TT], ft, ALU.mult, [b_bank[pu], bft],
                   [b_act[j]] + ([b_rden2] if j >= 20 else []))
                PS.put(pu)
        for c in range(DC):
            d0, bd0 = wget()
            d1, bd1 = wget()
            py = PS.get()
            for j in range(NJ):
                dsl, bd = (d0, bd0) if j < 11 else (d1, bd1)
                jj = j % 11
                mm(banks[py][:, 0:TT], dsl[:, jj * 128:(jj + 1) * 128], arena[:, j * TT:(j + 1) * TT], j == 0, j == NJ - 1,
                   [bd, b_act[j]] + ([b_rden2] if j >= 20 else []), [b_bank[py]])
            stt(xT[:, c, sl], banks[py][:, 0:TT], modcol(l, 5, c, bq), xT[:, c, sl], ALU.mult, ALU.add,
                [b_bank[py], b_modraw, b_xT[c][t]], [b_xT[c][t]])
            PS.put(py)

    for bq in range(NSEQ):
        for t in range(NT):
            sl = slice(t * TT, (t + 1) * TT)
            pi, bpi = poolF.get()
            pos_i = pi.bitcast(I32)
            ld(pos_i, pos_d[bq:bq + 1, sl].partition_broadcast(128), [], [bpi], bpi)
            ang, bang = poolF.get()
            cp("dve", ang, pos_i, [bpi], [bang])
            ts("dve", ang, ang, consts[:, C_INVF:C_INVF + 1], None, ALU.mult, ALU.bypass, [bang, b_const], [bang])
            for which, shift in ((1, 0.0), (0, math.pi / 2)):
                u, bu = poolF.get()
                ki, bki = poolF.get()
                ki_i = ki.bitcast(I32)
                ts("dve", u, ang, 1.0 / TWO_PI, 0.5 + shift / TWO_PI, ALU.mult, ALU.add, [bang], [bu])
                cp("dve", ki_i, u, [bu], [bki])
                cp("dve", u, ki_i, [bki], [bu])
                stt(u, u, -TWO_PI, ang, ALU.mult, ALU.add, [bu, bang], [bu])
                if shift:
                    ts("dve", u, u, shift, None, ALU.add, ALU.bypass, [bu], [bu])
                ts("dve", ki, u, math.pi, -TWO_PI, ALU.is_gt, ALU.mult, [bu], [bki])
                tt("dve", u, u, ki, ALU.add, [bu, bki], [bu])
                ts("dve", ki, u, -math.pi, TWO_PI, ALU.is_lt, ALU.mult, [bu], [bki])
                tt("dve", u, u, ki, ALU.add, [bu, bki], [bu])
                act(ki, u, AF.Sin, [bu], [bki])
                P.dma(lambda e, ki=ki, which=which, sl=sl, bq=bq: e.dma_start(out=cs_scr[bq, which, :, sl], in_=ki), reads=[bki],
                      writes=[b_csscr[bq]], sem_buf=bki)
        for blk in range(NB):
            t = blk // BPT
            xs, bxs = xstg.get()
            ld(xs, x_d[bq, blk * 128:(blk + 1) * 128, :], [], [bxs], bxs)
            for half in range(2):
                pb = PS.get()
                for cc in range(4):
                    c = half * 4 + cc
                    mm(banks[pb][:, cc * 128:(cc + 1) * 128], xs[:, c * 128:(c + 1) * 128], ident_f, True, True, [bxs, b_const], [b_bank[pb]])
                cp("act" if half else "dve", xT[:, half * 4:half * 4 + 4, blk * 128:(blk + 1) * 128],
                   banks[pb][:, 0:512].rearrange("p (c n) -> p c n", n=128), [b_bank[pb]], [b_xT[half * 4 + cc][t] for cc in range(4)])
                PS.put(pb)
        P.barrier()
        for l in range(L):
            ld(gnbc[:], gn_d[l:l + 1, :].partition_broadcast(128), [], [b_gn], b_gn)
            P.op("pool", lambda e: e.memset(stf[:], 0.0), [], b_stfh)
            P.op("pool", lambda e: e.memset(stb[:], 0.0), [], [b_stb])
            P.op("pool", lambda e: e.memset(carry[:], 0.0), [], b_carry)
            P.op("pool", lambda e: e.memset(uhist[:], 0.0), [], b_uh)
            for t in range(NT):
                tile_body(bq, l, t)
        P.barrier()
        for t in range(NT):
            rms_mod(t, lambda c: modA[:, L * 2, c, bq:bq + 1], lambda c: modraw[:, L * 48 + c, bq:bq + 1],
                    lambda c, t=t: xT[:, c, t * TT:(t + 1) * TT], [b_xT[c][t] for c in range(DC)])
            for n in range(BPT):
                blk = t * BPT + n
                xs, bxs = xstg.get()
                for half in range(2):
                    pb = PS.get()
                    for cc in range(4):
                        c = half * 4 + cc
                        mm(banks[pb][:, cc * 128:(cc + 1) * 128], xT[:, c, blk * 128:(blk + 1) * 128], ident_f, True, True,
                           [b_xT[c][t], b_const], [b_bank[pb]])
                    cp("act" if half else "dve", xs[:, half * 512:(half + 1) * 512], banks[pb][:, 0:512], [b_bank[pb]], [bxs])
                    PS.put(pb)
                out_stores.append(P.dma(lambda e, xs=xs, blk=blk, bq=bq: e.dma_start(out=out_d[bq, blk * 128:(blk + 1) * 128, :], in_=xs),
                                        reads=[bxs], sem_buf=bxs))
        P.barrier()
    P.wait_all("sp", out_stores)
    P.emit()
    print("[build] sbuf bytes remaining/partition:", nc.sbuf_bytes_remaining, " ops:", {e: len(v) for e, v in P.ops.items()}, flush=True)
    return nc


def pack_inputs(inp, cfg, n_cores):
    L, NSEQ = cfg.L, cfg.NSEQ
    f = lambda k: np.ascontiguousarray(np.asarray(inp[k]))
    vecs = np.zeros((128, L, NV), np.float32)

    def colmaj(v, n):
        return np.asarray(v, np.float32).reshape(n, 128).T

    for l in range(L):
        vecs[:, l, V_N1:V_N1 + 8] = colmaj(inp["norm1"][l], 8)
        vecs[:, l, V_N2:V_N2 + 8] = colmaj(inp["norm2"][l], 8)
        vecs[:, l, V_QN:V_QN + 2] = colmaj(inp["mla_q_norm"][l], 2)
        vecs[:, l, V_KVN:V_KVN + 1] = colmaj(inp["mla_kv_norm"][l], 1)
        cw = np.asarray(inp["lru_conv_w"][l], np.float32)
        for c in range(3):
            for j in range(4):
                vecs[:, l, V_CW + c * 4 + j] = cw[j, c * 128:(c + 1) * 128]
        vecs[:, l, V_CB:V_CB + 3] = colmaj(inp["lru_conv_b"][l], 3)
        vecs[:, l, V_BA:V_BA + 3] = colmaj(inp["lru_b_a"][l], 3)
        vecs[:, l, V_BI:V_BI + 3] = colmaj(inp["lru_b_i"][l], 3)
        vecs[:, l, V_LAM:V_LAM + 3] = colmaj(inp["lru_lambda"][l], 3)
        vecs[:, l, V_MB:V_MB + 48] = colmaj(inp["mod_b"][l], 48)
    fvec = np.zeros((128, 24), np.float32)
    fvec[:, 0:8] = colmaj(inp["final_norm"], 8)
    fvec[:, 8:24] = colmaj(inp["final_mod_b"], 16)
    shared = {
        "mod_w": f("mod_w"), "w_in": f("w_in"), "w_uq": f("mla_w_uq"), "w_ukv": f("mla_w_ukv"),
        "lru_w_a": f("lru_w_a"), "lru_w_i": f("lru_w_i"), "w_out": f("w_out"), "w_gu": f("w_gate_up"),
        "w_down": f("w_down"), "fmod_w": f("final_mod_w"), "vecs": vecs, "fvec": fvec, "ret_gn": f("ret_gn"),
        "consts": make_consts(),
    }
    x, c, pos = np.asarray(inp["x"]), np.asarray(inp["c"]), np.asarray(inp["positions"])
    maps = []
    for i in range(n_cores):
        sl = slice(i * NSEQ, (i + 1) * NSEQ)
        m = dict(shared)
        m["x"] = np.ascontiguousarray(x[sl])
        m["cT"] = np.ascontiguousarray(c[sl].reshape(NSEQ, 8, 128).transpose(2, 1, 0))
        m["pos"] = np.ascontiguousarray(pos[sl].astype(np.int32))
        maps.append(m)
    return maps


_NC_CACHE = {}
SEQ_PER_LAUNCH = 2


def kernel(**inputs):
    n_cores = 8
    nseq = SEQ_PER_LAUNCH
    cfg = Cfg(NSEQ=nseq)
    if nseq not in _NC_CACHE:
        _NC_CACHE[nseq] = build_program(cfg)
    nc = _NC_CACHE[nseq]
    B = np.asarray(inputs["x"]).shape[0]
    outs = []
    per = n_cores * nseq
    for part in range(B // per):
        sub = dict(inputs)
        sl = slice(part * per, (part + 1) * per)
        sub["x"] = np.asarray(inputs["x"])[sl]
        sub["c"] = np.asarray(inputs["c"])[sl]
        sub["positions"] = np.asarray(inputs["positions"])[sl]
        maps = pack_inputs(sub, cfg, n_cores)
        res = run_bass_kernel_spmd(nc, maps, core_ids=list(range(n_cores)))
        outs += [np.asarray(r["out"]) for r in res.results]
    return np.concatenate(outs, axis=0).astype(np.float32)
```

```python
import math
import numpy as np
import concourse.bass as bass
import concourse.mybir as mybir
from concourse.bass_utils import run_bass_kernel_spmd

F32 = mybir.dt.float32
BF16 = mybir.dt.bfloat16
I32 = mybir.dt.int32
AF = mybir.ActivationFunctionType
ALU = mybir.AluOpType
AX = mybir.AxisListType


class Buf:
    __slots__ = ("name", "last_w", "readers", "dma_sem", "dma_cnt", "excl")

    def __init__(self, name, excl=False):
        self.name = name
        self.excl = excl
        self.last_w = None
        self.readers = []
        self.dma_sem = None
        self.dma_cnt = 0


class Op:
    __slots__ = ("eng", "fn", "deps", "signal", "val", "sem", "is_dma", "idx")

    def __init__(self, eng, fn, is_dma=False):
        self.eng = eng
        self.fn = fn
        self.deps = []
        self.signal = False
        self.val = None
        self.sem = None
        self.is_dma = is_dma
        self.idx = -1


ENGS = ("pe", "act", "dve", "pool", "sp")


class Prog:
    def __init__(self, nc, stack):
        self.nc = nc
        self.stack = stack
        self.ops = {e: [] for e in ENGS}
        self.all_ops = []
        self.n_dma_sems = 0
        self.last_dma = {}

    def buf(self, name):
        return Buf(name)

    def _add(self, op, reads, writes):
        deps = []
        for b in reads:
            w = b.last_w
            if w is not None:
                deps.append(w)
            if b.excl:
                for r in b.readers:
                    if r.eng != op.eng:
                        deps.append(r)
            b.readers.append(op)
        for b in writes:
            w = b.last_w
            if w is not None and (w.eng != op.eng or op.is_dma or w.is_dma or op.eng != "pe"):
                if not (w.is_dma and op.is_dma and w.sem is op.sem):
                    deps.append(w)
            for r in b.readers:
                if r is not op and (r.eng != op.eng or op.is_dma or r.is_dma or op.eng != "pe"):
                    deps.append(r)
            b.last_w = op
            b.readers = []
        seen = set()
        for d in deps:
            if id(d) not in seen and d is not op:
                seen.add(id(d))
                op.deps.append(d)
                d.signal = True
        op.idx = len(self.all_ops)
        self.all_ops.append(op)
        self.ops[op.eng].append(op)
        return op

    def op(self, eng, fn, reads=(), writes=()):
        return self._add(Op(eng, fn), reads, writes)

    def dma(self, fn, reads=(), writes=(), sem_buf=None, eng="sp"):
        op = Op(eng, fn, is_dma=True)
        op.sem = sem_buf
        op.signal = True
        self.last_dma[id(sem_buf)] = op
        return self._add(op, reads, writes)

    def barrier(self):
        lasts = [self.ops[e][-1] for e in ENGS if self.ops[e]] + list(self.last_dma.values())
        for e in ENGS:
            op = Op(e, None)
            for l in lasts:
                if l.eng != e or l.is_dma:
                    op.deps.append(l)
                    l.signal = True
            op.idx = len(self.all_ops)
            self.all_ops.append(op)
            self.ops[e].append(op)

    def wait_all(self, eng, ops):
        op = Op(eng, None)
        for o in ops:
            op.deps.append(o)
            o.signal = True
        op.idx = len(self.all_ops)
        self.all_ops.append(op)
        self.ops[eng].append(op)

    def emit(self):
        nc = self.nc
        esem = {e: self.stack.enter_context(nc.semaphore("c_" + e)) for e in ENGS}
        for op in self.all_ops:
            if op.is_dma:
                b = op.sem
                if b.dma_sem is None:
                    b.dma_sem = self.stack.enter_context(nc.semaphore("d%d" % self.n_dma_sems))
                    self.n_dma_sems += 1
                b.dma_cnt += 16
                op.val = b.dma_cnt
                op.sem = b.dma_sem
        for e in ENGS:
            cnt = 0
            for op in self.ops[e]:
                if op.is_dma:
                    continue
                if op.signal and op.fn is not None:
                    cnt += 1
                    op.val = cnt
                    op.sem = esem[e]
        self.stats = {e: [0, 0] for e in ENGS}

        def resolve(d, out):
            if d.fn is None:
                for dd in d.deps:
                    resolve(dd, out)
            else:
                out.append(d)

        def stream(e, engobj):
            waited = {}
            for op in self.ops[e]:
                need = {}
                flat = []
                for d in op.deps:
                    resolve(d, flat)
                for d in flat:
                    k = id(d.sem)
                    if waited.get(k, 0) >= d.val:
                        continue
                    if k not in need or need[k][1] < d.val:
                        need[k] = (d.sem, d.val)
                for k, (s, v) in need.items():
                    engobj.wait_ge(s, v)
                    waited[k] = v
                    self.stats[e][1] += 1
                if op.fn is None:
                    continue
                inst = op.fn(engobj)
                self.stats[e][0] += 1
                if op.is_dma:
                    inst.then_inc(op.sem, 16)
                elif op.signal:
                    inst.then_inc(op.sem, 1)

        with nc.Block() as block:
            @block.sync
            def _(e):
                stream("sp", e)

            @block.tensor
            def _(e):
                stream("pe", e)

            @block.scalar
            def _(e):
                stream("act", e)

            @block.vector
            def _(e):
                stream("dve", e)

            @block.gpsimd
            def _(e):
                stream("pool", e)


D = 1024
DC = 8
HM, NOPE, ROPE, VD = 6, 64, 32, 64
HR, DK, DV = 4, 32, 64
LW = 384
FF = 2816
NJ = 22
D_IN = 1952
EPS = 1e-6
NPIECE = 58
SLOT = 2048
NV = 91
V_N1, V_N2, V_QN, V_KVN, V_CW, V_CB, V_BA, V_BI, V_LAM, V_MB = 0, 8, 16, 18, 19, 31, 34, 37, 40, 43
C_ID, C_TRI, C_DEC, C_QD, C_KD, C_INVF, C_CD, NCONST = 0, 128, 256, 768, 896, 1024, 1025, 1026


class Cfg:
    def __init__(self, S=2048, NSEQ=2, L=4, TT=512):
        self.S, self.NSEQ, self.L, self.TT = S, NSEQ, L, TT


def make_consts():
    c = np.zeros((128, NCONST), np.float32)
    c[:, C_ID:C_ID + 128] = np.eye(128, dtype=np.float32)
    k = np.arange(128)[:, None]
    q = np.arange(128)[None, :]
    c[:, C_TRI:C_TRI + 128] = (q >= k).astype(np.float32)
    log_g = np.log(1.0 - np.exp2(-5.0 - np.arange(HR, dtype=np.float32))).astype(np.float32)
    sc = np.float32(DK ** -0.5)
    for h in range(HR):
        diff = (q - k).astype(np.float32)
        dec = np.where(diff >= 0, np.exp(log_g[h] * np.maximum(diff, 0.0)), 0.0).astype(np.float32)
        c[:, C_DEC + h * 128:C_DEC + (h + 1) * 128] = dec * sc
        c[32 * h:32 * h + 32, C_QD:C_QD + 128] = np.exp(log_g[h] * (np.arange(128, dtype=np.float32) + 1.0))[None, :]
        c[:, C_KD + 32 * h:C_KD + 32 * h + 32] = (np.exp(log_g[h] * (127.0 - np.arange(128, dtype=np.float32))) * sc)[:, None]
        c[32 * h:32 * h + 32, C_CD] = np.exp(log_g[h] * 128.0)
    inv = (10000.0 ** (-np.arange(16, dtype=np.float32) / 16.0)).astype(np.float32)
    c[:, C_INVF] = inv[np.arange(128) % 16]
    return c


def build_program(cfg):
    from contextlib import ExitStack
    S, NSEQ, L, TT = cfg.S, cfg.NSEQ, cfg.L, cfg.TT
    NT, NB, BPT = S // TT, S // 128, TT // 128
    NSLOT = 6
    nc = bass.Bass("TRN2", target_bir_lowering=False)
    stack = ExitStack()
    P = Prog(nc, stack)

    def din(name, shape, dt=F32):
        return nc.dram_tensor(name, shape, dt, kind="ExternalInput").ap()

    x_d = din("x", [NSEQ, S, D])
    cT_d = din("cT", [128, DC, NSEQ])
    pos_d = din("pos", [NSEQ, S], I32)
    modw_d = din("mod_w", [L, D, 6 * D])
    win_d = din("w_in", [L, D, D_IN])
    wuq_d = din("w_uq", [L, 256, 576])
    wukv_d = din("w_ukv", [L, 128, 768])
    wa_d = din("lru_w_a", [L, 6, 64, 64])
    wi_d = din("lru_w_i", [L, 6, 64, 64])
    wout_d = din("w_out", [L, D, D])
    wgu_d = din("w_gu", [L, D, 2 * FF])
    wdn_d = din("w_down", [L, FF, D])
    fmodw_d = din("fmod_w", [D, 2 * D])
    vecs_d = din("vecs", [128, L, NV])
    fvec_d = din("fvec", [128, 24])
    gn_d = din("ret_gn", [L, 256])
    consts_d = din("consts", [128, NCONST])
    out_d = nc.dram_tensor("out", [NSEQ, S, D], F32, kind="ExternalOutput").ap()
    wscr = nc.dram_tensor("wscr", [L, NPIECE, 128, SLOT], BF16).ap()
    cs_scr = nc.dram_tensor("cs_scr", [NSEQ, 2, 128, S], F32).ap()

    def sb(name, shape, dt=F32):
        return stack.enter_context(nc.sbuf_tensor("s_" + name, shape, dt))

    SK = max(S, 2048)
    SX = max(S, 2048)
    xT = sb("xT", [128, DC, SX])
    kT = sb("kT", [128, HM, SK], BF16)
    Vc = sb("Vc", [128, NB, HM * 65], BF16)
    hT = sb("hT", [128, DC, TT], BF16)
    arena = sb("arena", [128, NJ * 512], BF16)
    arena_f = arena[:].bitcast(F32)
    mixA = sb("mixA", [64, HM, TT], BF16)
    mixB = sb("mixB", [128, 2, TT], BF16)
    mixC = sb("mixC", [128, 3, TT], BF16)
    ring = sb("ring", [128, NSLOT, SLOT], BF16)
    consts = sb("consts", [128, NCONST])
    identb = sb("identb", [128, 128], BF16)
    trib = sb("trib", [128, 128], BF16)
    onesb = sb("onesb", [128, 128], BF16)
    vecs = sb("vecs", [128, L, NV])
    fvec = sb("fvec", [128, 24])
    gnbc = sb("gnbc", [128, 256])
    cact = sb("cact", [128, DC, NSEQ])
    modraw = sb("modraw", [128, L * 48 + 16, NSEQ])
    modA = sb("modA", [128, L * 2 + 1, DC, NSEQ])
    cneg = sb("cneg", [128, L, 6])
    cossin = sb("cossin", [128, 2, TT])
    cqn = sb("cqn", [128, 2, TT], BF16)
    ckvn = sb("ckvn", [128, TT], BF16)
    qTall = sb("qTall", [128, 2, TT], BF16)
    rqd = sb("rqd", [128, TT], BF16)
    rkT = sb("rkT", [128, TT], BF16)
    qbd = sb("qbd", [128, BPT, HR, 128], BF16)
    uxb = sb("uxb", [128, 3, TT + 3])
    uhist = sb("uhist", [128, 3, 3])
    gel = sb("gel", [128, 3, TT], BF16)
    rstd = [sb("rstd%d" % i, [128, TT]) for i in range(2)]
    stf = sb("stf", [128, 256])
    stb = sb("stb", [128, 256], BF16)
    carry = sb("carry", [128, 3])
    small = sb("small", [128, 64])
    small2 = [sb("small2_%d" % i, [128, 32]) for i in range(2)]
    sgs = sb("sgs", [128, 3, 256], BF16)
    b_sgs = [Buf("sg%d" % i) for i in range(3)]
    banks = [stack.enter_context(nc.psum_tensor("ps%d" % i, [128, 512], F32)) for i in range(8)]

    B = P.buf
    b_xT = [[B("x%d_%d" % (c, t)) for t in range(NT)] for c in range(DC)]
    b_kT = [[B("k%d_%d" % (h, t)) for t in range(NT)] for h in range(HM)]
    b_V = [B("V%d" % t) for t in range(NT)]
    b_hT = [B("h%d" % c) for c in range(DC)]
    b_mixA = [B("mA%d" % h) for h in range(HM)]
    b_mixB = [B("mB%d" % i) for i in range(2)]
    b_mixC = [B("mC%d" % i) for i in range(3)]
    b_ring = [B("ring%d" % i) for i in range(NSLOT)]
    b_bank = [Buf("bank%d" % i, excl=True) for i in range(8)]
    b_const, b_identb, b_trib, b_onesb = B("consts"), B("identb"), B("trib"), B("onesb")
    b_vecs, b_fvec, b_gn, b_cact, b_modraw, b_modA, b_cneg = B("vecs"), B("fvec"), B("gn"), B("cact"), B("modraw"), B("modA"), B("cneg")
    b_cs = B("cossin")
    b_cqn, b_ckvn = [B("cqn0"), B("cqn1")], B("ckvn")
    b_qT = [B("qT%d" % h) for h in range(2)]
    b_rqd, b_rkT, b_qbd = B("rqd"), B("rkT"), B("qbd")
    b_ux = [B("ux%d" % c) for c in range(3)]
    b_uh = [B("uh%d" % c) for c in range(3)]
    b_gel = [B("gel%d" % c) for c in range(3)]
    b_rstd = [B("rstd0"), B("rstd1")]
    b_small2 = [B("small2_0"), B("small2_1")]
    b_stfh = [B("stf_h%d" % h) for h in range(HR)]
    b_stf, b_stb, b_carry, b_small = B("stf"), B("stb"), [B("carry%d" % c) for c in range(3)], B("small")
    b_wscr = [[B("wscr%d_%d" % (l, p)) for p in range(NPIECE)] for l in range(L)]
    b_csscr = [B("csscr%d" % b) for b in range(NSEQ)]

    class Pool:
        def __init__(self, aps, name):
            self.aps = aps
            self.bufs = [B("%s%d" % (name, i)) for i in range(len(aps))]
            self.i = 0

        def get(self):
            k = self.i % len(self.aps)
            self.i += 1
            return self.aps[k], self.bufs[k]

    NF = 6
    poolF = Pool([arena_f[:, i * 512:(i + 1) * 512] for i in range(NF)], "F")
    poolH = Pool([arena[:, NF * 1024 + i * 512: NF * 1024 + (i + 1) * 512] for i in range(6)], "H")
    rden_ap = arena_f[:, 4608:5120]
    bcs_ap = arena_f[:, 5120:5632]
    b_rden, b_bcs = B("rden"), B("bcs")
    b_rden2 = B("rden2")
    rdens, b_rdens = [rden_ap, bcs_ap], [b_rden, b_rden2]
    xcb_t = sb("xcb_t", [128, TT], BF16)
    b_xcbt = B("xcb_t")
    b_act = [poolF.bufs[j // 2] for j in range(12)] + [poolH.bufs[j] for j in range(6)] + [b_rden, b_rden, b_bcs, b_bcs]
    kT_f = kT[:].rearrange("p h s -> p (h s)").bitcast(F32)
    xT_flat = xT[:].rearrange("p c s -> p (c s)")
    stgs = Pool([kT_f[:, i * 2048:(i + 1) * 2048] for i in range(3)], "stg")
    blks = [xT_flat[:, 0:8192], xT_flat[:, 8192:16384]]
    b_blk = [B("blkA"), B("blkB")]
    cvo = Pool([arena[:, i * 2048:(i + 1) * 2048] for i in range(5)], "cvo")

    class PS:
        free = list(range(8))

        @staticmethod
        def get():
            i = PS.free.pop(0)
            return i

        @staticmethod
        def put(i):
            PS.free.append(i)

    def mm(out, lhsT, rhs, start, stop, reads, writes, sgc=False):
        return P.op("pe", lambda e: e.matmul(out, lhsT=lhsT, rhs=rhs, start=start, stop=stop, skip_group_check=sgc), reads, writes)

    ldq = {"alt": False, "n": 0}

    def ld(out_ap, in_ap, reads, writes, sem_buf):
        eng = "sp"
        if ldq["alt"]:
            ldq["n"] += 1
            eng = "act" if ldq["n"] % 2 else "sp"
        return P.dma(lambda e: e.dma_start(out=out_ap, in_=in_ap), reads=reads, writes=writes, sem_buf=sem_buf, eng=eng)

    def act(out, in_, func, reads, writes, scale=1.0, bias=0.0):
        return P.op("act", lambda e: e.activation(out=out, in_=in_, func=func, scale=scale, bias=bias), reads, writes)

    def tt(eng, out, in0, in1, op, reads, writes):
        return P.op(eng, lambda e: e.tensor_tensor(out=out, in0=in0, in1=in1, op=op), reads, writes)

    def ts(eng, out, in0, s1, s2, op0, op1, reads, writes):
        return P.op(eng, lambda e: e.tensor_scalar(out=out, in0=in0, scalar1=s1, scalar2=s2, op0=op0, op1=op1), reads, writes)

    def stt(out, in0, scalar, in1, op0, op1, reads, writes):
        return P.op("dve", lambda e: e.scalar_tensor_tensor(out=out, in0=in0, scalar=scalar, in1=in1, op0=op0, op1=op1), reads, writes)

    def cp(eng, out, in_, reads, writes):
        if eng == "act":
            return P.op("act", lambda e: e.copy(out=out, in_=in_), reads, writes)
        return P.op(eng, lambda e: e.tensor_copy(out=out, in_=in_), reads, writes)

    ld(consts[:], consts_d, [], [b_const], b_const)
    ld(vecs[:], vecs_d, [], [b_vecs], b_vecs)
    ld(fvec[:], fvec_d, [], [b_fvec], b_fvec)
    ld(cact[:], cT_d, [], [b_cact], b_cact)
    cp("dve", identb[:], consts[:, C_ID:C_ID + 128], [b_const], [b_identb])
    cp("dve", trib[:], consts[:, C_TRI:C_TRI + 128], [b_const], [b_trib])
    P.op("pool", lambda e: e.memset(onesb[:], 1.0), [], [b_onesb])
    ones_f = sb("ones_f", [128, 64])
    b_onesf = B("ones_f")
    P.op("pool", lambda e: e.memset(ones_f[:], 1.0), [], [b_onesf])
    P.op("pool", lambda e: e.memset(Vc[:], 1.0), [], b_V)
    P.op("pool", lambda e: e.memset(qbd[:], 0.0), [], [b_qbd])
    act(cact[:], cact[:], AF.Silu, [b_cact], [b_cact])
    for l in range(L):
        act(small[:, 0:3], vecs[:, l, V_LAM:V_LAM + 3], AF.Exp, [b_vecs], [b_small], scale=-1.0)
        act(small[:, 3:6], small[:, 0:3], AF.Ln, [b_small], [b_small], bias=1.0)
        ts("dve", cneg[:, l, 0:3], small[:, 3:6], -8.0, None, ALU.mult, ALU.bypass, [b_small], [b_cneg])
        ts("dve", cneg[:, l, 3:6], small[:, 3:6], -16.0, None, ALU.mult, ALU.bypass, [b_small], [b_cneg])

    cast_rr = [0]

    def cast(dst, src, bs, bd, scale=None):
        eng = ("dve", "act", "dve", "act", "dve", "dve", "act", "dve", "pool", "dve", "act", "dve")[cast_rr[0] % 12]
        cast_rr[0] += 1
        if scale is None:
            cp(eng, dst, src, [bs], [bd])
        elif eng == "act":
            P.op("act", lambda e: e.mul(out=dst, in_=src, mul=scale), [bs], [bd])
        else:
            ts(eng, dst, src, scale, None, ALU.mult, ALU.bypass, [bs], [bd])

    cact_b = sb("cact_b", [128, DC, NSEQ], BF16)
    b_cactb = B("cact_b")
    cp("dve", cact_b[:], cact[:], [b_cact], [b_cactb])
    def mod_group(src_ap, col0, bias_ap_fn):
        pb = PS.get()
        for kc in range(DC):
            stg, bstg = stgs.get()
            ld(stg, src_ap[kc * 128:(kc + 1) * 128, :], [], [bstg], bstg)
            wb, bwb = cvo.get()
            cast(wb, stg, bstg, bwb)
            for cc in range(16):
                mm(banks[pb][:, cc * NSEQ:(cc + 1) * NSEQ], wb[:, cc * 128:(cc + 1) * 128], cact_b[:, kc, :],
                   kc == 0 and cc == 0, kc == DC - 1 and cc == 15, [bwb, b_cactb], [b_bank[pb]], sgc=True)
        for cc in range(16):
            act(modraw[:, col0 + cc, :], banks[pb][:, cc * NSEQ:(cc + 1) * NSEQ], AF.Identity, [b_bank[pb], b_vecs, b_fvec], [b_modraw],
                bias=bias_ap_fn(cc))
        PS.put(pb)

    ldq["alt"] = True
    for l in range(L):
        for g in range(3):
            mod_group(modw_d[l][:, g * 2048:(g + 1) * 2048], l * 48 + g * 16,
                      lambda cc, l=l, g=g: vecs[:, l, V_MB + g * 16 + cc: V_MB + g * 16 + cc + 1])
    mod_group(fmodw_d[:, 0:2048], L * 48, lambda cc: fvec[:, 8 + cc: 8 + cc + 1])
    for bq in range(NSEQ):
        for l in range(L):
            stt(modA[:, l * 2 + 0, :, bq], modraw[:, l * 48 + 8: l * 48 + 16, bq], 1.0, vecs[:, l, V_N1:V_N1 + 8], ALU.add, ALU.mult,
                [b_modraw, b_vecs], [b_modA])
            stt(modA[:, l * 2 + 1, :, bq], modraw[:, l * 48 + 32: l * 48 + 40, bq], 1.0, vecs[:, l, V_N2:V_N2 + 8], ALU.add, ALU.mult,
                [b_modraw, b_vecs], [b_modA])
        stt(modA[:, L * 2, :, bq], modraw[:, L * 48 + 8: L * 48 + 16, bq], 1.0, fvec[:, 0:8], ALU.add, ALU.mult,
            [b_modraw, b_fvec], [b_modA])

    def modcol(l, j, c, bq):
        return modraw[:, l * 48 + j * 8 + c, bq:bq + 1]

    def sw(dst_of, src_of, bs, bd):
        cast(dst_of(0, 16), src_of(16, 32), bs, bd, scale=-1.0)
        cast(dst_of(16, 32), src_of(0, 16), bs, bd)

    def v3(ap):
        return ap.rearrange("p (k n) -> p k n", n=256)

    def convert_piece(l, p):
        stg, bstg = stgs.get()
        slot, bsl = cvo.get()
        win = win_d[l].rearrange("(k p) n -> p k n", p=128)
        s3, o3 = v3(stg), v3(slot)
        zero = False

        def L_(dst, src):
            ld(dst, src, [], [bstg], bstg)

        def Z():
            P.op("pool", lambda e: e.memset(slot, 0.0), [], [bsl])

        if p == 0:
            L_(s3, win[:, :, 0:256]); cast(slot, stg, bstg, bsl)
        elif p == 1:
            L_(s3[:, :, 0:128], win[:, :, 256:384]); L_(s3[:, :, 128:160], win[:, :, 384:416]); Z()
            cast(o3[:, :, 0:128], s3[:, :, 0:128], bstg, bsl); cast(o3[:, :, 192:224], s3[:, :, 128:160], bstg, bsl)
        elif p == 2:
            L_(s3[:, :, 0:32], win[:, :, 384:416]); L_(s3[:, :, 128:256], win[:, :, 416:544]); Z()
            sw(lambda a, b: o3[:, :, 64 + a:64 + b], lambda a, b: s3[:, :, a:b], bstg, bsl); cast(o3[:, :, 128:256], s3[:, :, 128:256], bstg, bsl)
        elif p in (3, 4):
            a0 = 416 if p == 3 else 544
            b0 = 544 if p == 3 else 1184
            L_(s3[:, :, 0:128], win[:, :, a0:a0 + 128]); L_(s3[:, :, 128:256], win[:, :, b0:b0 + 128])
            for h in range(HR):
                sw(lambda a, b, h=h: o3[:, :, 32 * h + a:32 * h + b], lambda a, b, h=h: s3[:, :, 32 * h + a:32 * h + b], bstg, bsl)
            cast(o3[:, :, 128:256], s3[:, :, 128:256], bstg, bsl)
        elif p in (5, 6, 8, 9):
            c0 = {5: 1312, 6: 1568, 8: 672, 9: 928}[p]
            L_(s3, win[:, :, c0:c0 + 256]); cast(slot, stg, bstg, bsl)
        elif p == 7:
            L_(s3[:, :, 0:128], win[:, :, 1824:1952]); Z(); cast(o3[:, :, 0:128], s3[:, :, 0:128], bstg, bsl)
        elif p == 10:
            L_(stg[:, 0:1152].rearrange("p (k n) -> p k n", n=576), wuq_d[l].rearrange("(k p) n -> p k n", p=128))
            cast(slot[:, 0:1152], stg[:, 0:1152], bstg, bsl)
            for kc in range(2):
                for h in range(HM):
                    o = 1152 + (kc * HM + h) * 32
                    i = kc * 576 + h * 96 + 64
                    sw(lambda a, b, o=o: slot[:, o + a:o + b], lambda a, b, i=i: stg[:, i + a:i + b], bstg, bsl)
        elif p == 11:
            L_(stg[:, 0:768], wukv_d[l])
            for g in range(6):
                r0, cc = (g % 2) * 64, g // 2
                L_(stg[r0:r0 + 64, 768 + cc * 128 + r0: 768 + cc * 128 + r0 + 64], wa_d[l, g])
                L_(stg[r0:r0 + 64, 1152 + cc * 128 + r0: 1152 + cc * 128 + r0 + 64], wi_d[l, g])
            Z()
            kv3 = stg[:, 0:768].rearrange("p (h n) -> p h n", n=128)
            cast(slot[:, 0:384].rearrange("p (h n) -> p h n", n=64), kv3[:, :, 0:64], bstg, bsl)
            cast(slot[:, 384:768].rearrange("p (h n) -> p h n", n=64), kv3[:, :, 64:128], bstg, bsl)
            for g in range(6):
                r0, cc = (g % 2) * 64, g // 2
                for base in (768, 1152):
                    o = base + cc * 128 + r0
                    cast(slot[r0:r0 + 64, o:o + 64], stg[r0:r0 + 64, o:o + 64], bstg, bsl)
        elif 12 <= p < 20:
            c = p - 12
            L_(stg[0:64, 0:768].rearrange("p (h n) -> p h n", n=128),
               wout_d[l][0:384, c * 128:(c + 1) * 128].rearrange("(h r) n -> r h n", r=64))
            L_(stg[:, 768:1408].rearrange("p (k n) -> p k n", n=128),
               wout_d[l][384:1024, c * 128:(c + 1) * 128].rearrange("(k p) n -> p k n", p=128))
            Z()
            cast(slot[0:64, 0:768], stg[0:64, 0:768], bstg, bsl); cast(slot[:, 768:1408], stg[:, 768:1408], bstg, bsl)
        elif 20 <= p < 42:
            jj, isup = (p - 20) // 2, (p - 20) % 2
            wgu = wgu_d[l].rearrange("(k p) n -> p k n", p=128)
            c0 = isup * FF + jj * 256
            L_(s3, wgu[:, :, c0:c0 + 256])
            cast(slot, stg, bstg, bsl)
        else:
            c, hf = (p - 42) // 2, (p - 42) % 2
            L_(stg[:, 0:1408].rearrange("p (j n) -> p j n", n=128),
               wdn_d[l][hf * 1408:(hf + 1) * 1408, c * 128:(c + 1) * 128].rearrange("(j p) n -> p j n", p=128))
            cast(slot[:, 0:1408], stg[:, 0:1408], bstg, bsl)
        P.dma(lambda e: e.dma_start(out=wscr[l, p], in_=slot), reads=[bsl], writes=[b_wscr[l][p]], sem_buf=bsl)

    def store_piece(l, p, slot, bsl):
        P.dma(lambda e: e.dma_start(out=wscr[l, p], in_=slot), reads=[bsl], writes=[b_wscr[l][p]], sem_buf=bsl)

    def conv_win(l):
        W0 = blks[0][:, 0:8 * 928].rearrange("p (k n) -> p k n", n=928)
        W1 = blks[1][:, 0:8192].rearrange("p (k n) -> p k n", n=1024)
        for kc in range(DC):
            ld(W0[:, kc, :], win_d[l][kc * 128:(kc + 1) * 128, 0:928], [], [b_blk[0]], b_blk[0])
            ld(W1[:, kc, :], win_d[l][kc * 128:(kc + 1) * 128, 928:1952], [], [b_blk[1]], b_blk[1])

        def src(a, b):
            return (W0[:, :, a:b], b_blk[0]) if b <= 928 else (W1[:, :, a - 928:b - 928], b_blk[1])

        def plain(o3, d0, a, b, bsl):
            sv, bs = src(a, b)
            cast(o3[:, :, d0:d0 + (b - a)], sv, bs, bsl)

        def swp(o3, d0, a, bsl):
            sv, bs = src(a, a + 32)
            cast(o3[:, :, d0:d0 + 16], sv[:, :, 16:32], bs, bsl, scale=-1.0)
            cast(o3[:, :, d0 + 16:d0 + 32], sv[:, :, 0:16], bs, bsl)

        for p in range(10):
            slot, bsl = cvo.get()
            o3 = v3(slot)
            if p in (1, 2, 7):
                P.op("pool", lambda e, slot=slot: e.memset(slot, 0.0), [], [bsl])
            if p == 0:
                plain(o3, 0, 0, 256, bsl)
            elif p == 1:
                plain(o3, 0, 256, 384, bsl); plain(o3, 192, 384, 416, bsl)
            elif p == 2:
                swp(o3, 64, 384, bsl); plain(o3, 128, 416, 544, bsl)
            elif p in (3, 4):
                a0 = 416 if p == 3 else 544
                b0 = 544 if p == 3 else 1184
                for h in range(HR):
                    swp(o3, 32 * h, a0 + 32 * h, bsl)
                plain(o3, 128, b0, b0 + 128, bsl)
            elif p in (5, 6, 8, 9):
                c0 = {5: 1312, 6: 1568, 8: 672, 9: 928}[p]
                plain(o3, 0, c0, c0 + 256, bsl)
            elif p == 7:
                plain(o3, 0, 1824, 1952, bsl)
            store_piece(l, p, slot, bsl)

    def conv_gu(l):
        for blk in range(6):
            c0 = blk * 1024
            n = min(1024, 2 * FF - c0)
            st, bst = blks[blk % 2], b_blk[blk % 2]
            v = st[:, 0:8 * n].rearrange("p (k n) -> p k n", n=n)
            for kc in range(DC):
                ld(v[:, kc, :], wgu_d[l][kc * 128:(kc + 1) * 128, c0:c0 + n], [], [bst], bst)
            for off in range(0, n, 256):
                col = c0 + off
                isup = 1 if col >= FF else 0
                jj = (col - isup * FF) // 256
                slot, bsl = cvo.get()
                cast(v3(slot), v[:, :, off:off + 256], bst, bsl)
                store_piece(l, 20 + 2 * jj + isup, slot, bsl)

    def conv_d(l):
        for hf in range(2):
            for cg in range(2):
                k = hf * 2 + cg
                st, bst = blks[k % 2], b_blk[k % 2]
                v = st[:, 0:11 * 512].rearrange("p (j n) -> p j n", n=512)
                for j in range(11):
                    r0 = (hf * 11 + j) * 128
                    ld(v[:, j, :], wdn_d[l][r0:r0 + 128, cg * 512:(cg + 1) * 512], [], [bst], bst)
                for cc in range(4):
                    c = cg * 4 + cc
                    slot, bsl = cvo.get()
                    cast(slot[:, 0:1408].rearrange("p (j n) -> p j n", n=128), v[:, :, cc * 128:(cc + 1) * 128], bst, bsl)
                    store_piece(l, 42 + 2 * c + hf, slot, bsl)

    for l in range(L):
        conv_win(l)
        for p in range(10, 20):
            convert_piece(l, p)
        conv_gu(l)
        conv_d(l)
    ldq["alt"] = False
    P.barrier()

    seq_list = [(l, p) for _b in range(NSEQ) for l in range(L) for _t in range(NT) for p in range(NPIECE)]
    wstate = {"issued": 0, "cur": 0}

    def wget():
        k = wstate["cur"]
        wstate["cur"] += 1
        lim = min(len(seq_list), k + NSLOT - 1)
        while wstate["issued"] < lim:
            i = wstate["issued"]
            l_, p_ = seq_list[i]
            s_ = i % NSLOT
            ld(ring[:, s_, :], wscr[l_, p_], [b_wscr[l_][p_]], [b_ring[s_]], b_ring[s_])
            wstate["issued"] += 1
        return ring[:, k % NSLOT, :], b_ring[k % NSLOT]

    hT_f = hT[:].rearrange("p c t -> p (c t)").bitcast(F32)
    xstg = Pool([hT_f[:, i * 1024:(i + 1) * 1024] for i in range(2)], "xs")
    ident_f = consts[:, C_ID:C_ID + 128]
    ffn_tmp = Pool([rstd[0][:], rstd[1][:]], "ft")
    ffn_tmp.bufs[0] = b_rstd[0]
    ffn_tmp.bufs[1] = b_rstd[1]
    log_g = np.log(1.0 - np.exp2(-5.0 - np.arange(HR, dtype=np.float32))).astype(np.float32)
    CDEC = [float(np.exp(log_g[h] * 128.0)) for h in range(HR)]
    TWO_PI = 2.0 * math.pi
    out_stores = []

    def sumsq_rstd(srcs, nfeat, r, br):
        pb = PS.get()
        for i, (ap, bl) in enumerate(srcs):
            sq, bsq = poolH.get()
            act(sq, ap, AF.Square, bl, [bsq])
            mm(banks[pb][:, 0:TT], onesb[:], sq, i == 0, i == len(srcs) - 1, [b_onesb, bsq], [b_bank[pb]])
        act(r, banks[pb][:, 0:TT], AF.Ln, [b_bank[pb]], [br], scale=1.0 / nfeat, bias=EPS)
        act(r, r, AF.Exp, [br], [br], scale=-0.5)
        PS.put(pb)

    def rms_mod(t, scale_of, bias_of, out_of, out_bufs):
        sl = slice(t * TT, (t + 1) * TT)
        r, br = rstd[0][:], b_rstd[0]
        sumsq_rstd([(xT[:, c, sl], [b_xT[c][t]]) for c in range(DC)], D, r, br)
        for c in range(DC):
            tmp, btmp = poolF.get()
            tt("dve", tmp, xT[:, c, sl], r, ALU.mult, [b_xT[c][t], br], [btmp])
            act(out_of(c), tmp, AF.Identity, [btmp, b_modA, b_modraw], [out_bufs[c]], scale=scale_of(c), bias=bias_of(c))

    def proj_fm(slot, bsl, col0, M, pb):
        s3 = v3(slot)
        for kc in range(DC):
            mm(banks[pb][0:M, 0:TT], s3[:, kc, col0:col0 + M], hT[:, kc, :], kc == 0, kc == DC - 1, [bsl, b_hT[kc]], [b_bank[pb]])

    def rot(pa, pbk, rows, out_fn):
        t1, bt1 = poolF.get()
        t2, bt2 = poolF.get()
        tt("dve", t1[rows, :], banks[pa][rows, 0:TT], cossin[rows, 0, :], ALU.mult, [b_bank[pa], b_cs], [bt1])
        tt("dve", t2[rows, :], banks[pbk][rows, 0:TT], cossin[rows, 1, :], ALU.mult, [b_bank[pbk], b_cs], [bt2])
        out_fn(t1, t2, [bt1, bt2])

    def tile_body(bq, l, t):
        sl = slice(t * TT, (t + 1) * TT)
        P.dma(lambda e: e.dma_start(out=cossin[:, 0, :], in_=cs_scr[bq, 0, :, sl]), reads=[b_csscr[bq]], writes=[b_cs], sem_buf=b_cs, eng="pool")
        P.dma(lambda e: e.dma_start(out=cossin[:, 1, :], in_=cs_scr[bq, 1, :, sl]), reads=[b_csscr[bq]], writes=[b_cs], sem_buf=b_cs, eng="pool")
        rms_mod(t, lambda c: modA[:, l * 2, c, bq:bq + 1], lambda c: modcol(l, 0, c, bq), lambda c: hT[:, c, :], b_hT)
        s0, bs0 = wget()
        pq = [PS.get(), PS.get()]
        for kc in range(DC):
            for i in range(2):
                mm(banks[pq[i]][:, 0:TT], v3(s0)[:, kc, i * 128:(i + 1) * 128], hT[:, kc, :], kc == 0, kc == DC - 1,
                   [bs0, b_hT[kc]], [b_bank[pq[i]]])
        r1, br1 = rstd[1][:], b_rstd[1]
        sumsq_rstd([(banks[pq[i]][:, 0:TT], [b_bank[pq[i]]]) for i in range(2)], 256, r1, br1)
        for i in range(2):
            stt(cqn[:, i, :], banks[pq[i]][:, 0:TT], vecs[:, l, V_QN + i:V_QN + i + 1], r1, ALU.mult, ALU.mult,
                [b_bank[pq[i]], b_vecs, br1], [b_cqn[i]])
            PS.put(pq[i])
        s1, bs1 = wget()
        pkv, pkr = PS.get(), PS.get()
        proj_fm(s1, bs1, 0, 128, pkv)
        proj_fm(s1, bs1, 128, 96, pkr)
        sumsq_rstd([(banks[pkv][:, 0:TT], [b_bank[pkv]])], 128, r1, br1)
        stt(ckvn[:], banks[pkv][:, 0:TT], vecs[:, l, V_KVN:V_KVN + 1], r1, ALU.mult, ALU.mult, [b_bank[pkv], b_vecs, br1], [b_ckvn])
        PS.put(pkv)
        s2, bs2 = wget()
        pks, prq = PS.get(), PS.get()
        proj_fm(s2, bs2, 0, 96, pks)
        proj_fm(s2, bs2, 128, 128, prq)

        def kr_out(t1, t2, bl):
            for h in range(HM):
                tt("pool", kT[64:96, h, sl], t1[64:96, :], t2[64:96, :], ALU.add, bl, [b_kT[h][t]])
        rot(pkr, pks, slice(64, 96), kr_out)
        PS.put(pkr)
        PS.put(pks)
        s3_, bs3 = wget()
        prqs, prk = PS.get(), PS.get()
        proj_fm(s3_, bs3, 0, 128, prqs)
        proj_fm(s3_, bs3, 128, 128, prk)

        def rq_out(t1, t2, bl):
            tt("pool", t1, t1, t2, ALU.add, bl, [bl[0]])
            for n in range(BPT):
                tt("pool", rqd[:, n * 128:(n + 1) * 128], t1[:, n * 128:(n + 1) * 128], consts[:, C_QD:C_QD + 128], ALU.mult,
                   [bl[0], b_const], [b_rqd])
            for h in range(HR):
                cp("act" if h % 2 else "pool", qbd[32 * h:32 * h + 32, :, h, :],
                   t1[32 * h:32 * h + 32, :].rearrange("p (n c) -> p n c", c=128), [bl[0]], [b_qbd])
        rot(prq, prqs, slice(0, 128), rq_out)
        PS.put(prq)
        PS.put(prqs)
        s4, bs4 = wget()
        prks = PS.get()
        pux = [PS.get()]
        proj_fm(s4, bs4, 0, 128, prks)
        proj_fm(s4, bs4, 128, 128, pux[0])
        rot(prk, prks, slice(0, 128), lambda t1, t2, bl: tt("pool", rkT[:], t1, t2, ALU.add, bl, [b_rkT]))
        PS.put(prk)
        PS.put(prks)
        s5, bs5 = wget()
        pux += [PS.get(), PS.get()]
        proj_fm(s5, bs5, 0, 128, pux[1])
        proj_fm(s5, bs5, 128, 128, pux[2])
        for c in range(3):
            cp("act", uxb[:, c, 3:3 + TT], banks[pux[c]][:, 0:TT], [b_bank[pux[c]]], [b_ux[c]])
            cp("pool", uxb[:, c, 0:3], uhist[:, c, :], [b_uh[c]], [b_ux[c]])
            cp("pool", uhist[:, c, :], uxb[:, c, TT:TT + 3], [b_ux[c]], [b_uh[c]])
            PS.put(pux[c])
        s6, bs6 = wget()
        pug = [PS.get(), PS.get()]
        proj_fm(s6, bs6, 0, 128, pug[0])
        proj_fm(s6, bs6, 128, 128, pug[1])
        for c in range(2):
            act(gel[:, c, :], banks[pug[c]][:, 0:TT], AF.Gelu_apprx_tanh, [b_bank[pug[c]]], [b_gel[c]])
            PS.put(pug[c])
        s7, bs7 = wget()
        pg2 = PS.get()
        proj_fm(s7, bs7, 0, 128, pg2)
        act(gel[:, 2, :], banks[pg2][:, 0:TT], AF.Gelu_apprx_tanh, [b_bank[pg2]], [b_gel[2]])
        PS.put(pg2)
        s8, bs8 = wget()
        s9, bs9 = wget()
        def ret_fa(n):
            bs_ = slice(n * 128, (n + 1) * 128)
            pv, pg = PS.get(), PS.get()
            for kc in range(DC):
                mm(banks[pv][:, 0:256], hT[:, kc, bs_], v3(s8)[:, kc, :], kc == 0, kc == DC - 1, [b_hT[kc], bs8], [b_bank[pv]])
            for kc in range(DC):
                mm(banks[pg][:, 0:256], hT[:, kc, bs_], v3(s9)[:, kc, :], kc == 0, kc == DC - 1, [b_hT[kc], bs9], [b_bank[pg]])
            psc = PS.get()
            mm(banks[psc][:, 0:512], rkT[:, bs_], qbd[:, n, :, :].rearrange("p h c -> p (h c)"), True, True, [b_rkT, b_qbd], [b_bank[psc]])
            pkt = PS.get()
            mm(banks[pkt][:, 0:128], rkT[:, bs_], identb[:], True, True, [b_rkT, b_identb], [b_bank[pkt]])
            vsb, bvsb = poolH.get()
            cp("act", vsb[:, 0:256], banks[pv][:, 0:256], [b_bank[pv]], [bvsb])
            PS.put(pv)
            sg, bsg = sgs[:, n % 3, :], b_sgs[n % 3]
            act(sg, banks[pg][:, 0:256], AF.Silu, [b_bank[pg]], [bsg])
            PS.put(pg)
            tt("pool", sg, sg, gnbc[:], ALU.mult, [bsg, b_gn], [bsg])
            ptr, bptr = poolH.get()
            tt("dve", ptr, banks[psc][:, 0:512], consts[:, C_DEC:C_DEC + 512], ALU.mult, [b_bank[psc], b_const], [bptr])
            PS.put(psc)
            kdt, bkdt = poolH.get()
            tt("dve", kdt[:, 0:128], banks[pkt][:, 0:128], consts[:, C_KD:C_KD + 128], ALU.mult, [b_bank[pkt], b_const], [bkdt])
            PS.put(pkt)
            return (n, vsb, bvsb, sg, bsg, ptr, bptr, kdt, bkdt)

        def ret_fb(fa):
            n, vsb, bvsb, sg, bsg, ptr, bptr, kdt, bkdt = fa
            bs_ = slice(n * 128, (n + 1) * 128)
            po = PS.get()
            mm(banks[po][:, 0:256], rqd[:, bs_], stb[:], True, False, [b_rqd, b_stb], [b_bank[po]])
            for h in range(HR):
                mm(banks[po][:, h * 64:(h + 1) * 64], ptr[:, h * 128:(h + 1) * 128], vsb[:, h * 64:(h + 1) * 64], False, h == HR - 1,
                   [bptr, bvsb], [b_bank[po]])
            pk2 = PS.get()
            mm(banks[pk2][:, 0:256], kdt[:, 0:128], vsb[:, 0:256], True, True, [bkdt, bvsb], [b_bank[pk2]])
            for h in range(HR):
                rs, cs = slice(32 * h, 32 * h + 32), slice(64 * h, 64 * h + 64)
                stt(stf[rs, cs], stf[rs, cs], CDEC[h], banks[pk2][rs, cs], ALU.mult, ALU.add, [b_stfh[h], b_bank[pk2]], [b_stfh[h]])
            PS.put(pk2)
            cp("pool", stb[:], stf[:], b_stfh, [b_stb])
            return n, po, sg, bsg

        def ret_tail(st):
            n, po, sg, bsg = st
            bs_ = slice(n * 128, (n + 1) * 128)
            sm, bsm = small2[n % 2], b_small2[n % 2]
            osb, bosb = poolF.get()
            cp("act", osb[:, 0:256], banks[po][:, 0:256], [b_bank[po]], [bosb])
            PS.put(po)
            o3 = osb[:, 0:256].rearrange("p (h d) -> p h d", d=64)
            P.op("dve", lambda e, o3=o3, sm=sm: e.reduce_sum(out=sm[:, 0:4], in_=o3, axis=AX.X), [bosb], [bsm])
            osq, bosq = poolF.get()
            act(osq[:, 0:256], osb[:, 0:256], AF.Square, [bosb], [bosq])
            osq3 = osq[:, 0:256].rearrange("p (h d) -> p h d", d=64)
            P.op("dve", lambda e, osq3=osq3, sm=sm: e.reduce_sum(out=sm[:, 4:8], in_=osq3, axis=AX.X), [bosq], [bsm])
            ts("dve", sm[:, 8:12], sm[:, 0:4], 1.0 / 64, None, ALU.mult, ALU.bypass, [bsm], [bsm])
            tt("dve", sm[:, 12:16], sm[:, 8:12], sm[:, 8:12], ALU.mult, [bsm], [bsm])
            stt(sm[:, 16:20], sm[:, 4:8], 1.0 / 64, sm[:, 12:16], ALU.mult, ALU.subtract, [bsm], [bsm])
            act(sm[:, 20:24], sm[:, 16:20], AF.Ln, [bsm], [bsm], bias=EPS)
            act(sm[:, 20:24], sm[:, 20:24], AF.Exp, [bsm], [bsm], scale=-0.5)
            on = osq
            eph = [B("gn_eph%d" % h) for h in range(HR)]
            for h in range(HR):
                ts("dve", on[:, h * 64:(h + 1) * 64], osb[:, h * 64:(h + 1) * 64], sm[:, 8 + h:9 + h], sm[:, 20 + h:21 + h],
                   ALU.subtract, ALU.mult, [bosb, bsm, bosq], [eph[h]])
            ob, bob = poolH.get()
            tt("pool", ob[:, 0:256], on[:, 0:256], sg, ALU.mult, [bosq, bsg] + eph, [bob])
            pt_ = PS.get()
            for i in range(2):
                mm(banks[pt_][:, i * 128:(i + 1) * 128], ob[:, i * 128:(i + 1) * 128], identb[:], True, True, [bob, b_identb], [b_bank[pt_]])
            for i in range(2):
                cp("act", mixB[:, i, bs_], banks[pt_][:, i * 128:(i + 1) * 128], [b_bank[pt_]], [b_mixB[i]])
            PS.put(pt_)

        fas = {0: ret_fa(0)}
        fbs = {}
        for n in range(BPT):
            fbs[n] = ret_fb(fas.pop(n))
            if n + 1 < BPT:
                fas[n + 1] = ret_fa(n + 1)
            if n >= 1:
                ret_tail(fbs.pop(n - 1))
        ret_tail(fbs.pop(BPT - 1))
        sa, bsa = wget()
        sbw, bsb = wget()
        for h in range(HM):
            pk = PS.get()
            mm(banks[pk][0:64, 0:TT], sbw[:, h * 64:(h + 1) * 64], ckvn[:], True, True, [bsb, b_ckvn], [b_bank[pk]])
            cp("act" if h % 2 else "dve", kT[0:64, h, sl], banks[pk][0:64, 0:TT], [b_bank[pk]], [b_kT[h][t]])
            PS.put(pk)
        for n in range(BPT):
            pvv = PS.get()
            mm(banks[pvv][:, 0:384], ckvn[:, n * 128:(n + 1) * 128], sbw[:, 384:768], True, True, [b_ckvn, bsb], [b_bank[pvv]])
            cp("dve", Vc[:, t * BPT + n, :].rearrange("p (h d) -> p h d", d=65)[:, :, 0:64],
               banks[pvv][:, 0:384].rearrange("p (h d) -> p h d", d=64), [b_bank[pvv]], [b_V[t]])
            PS.put(pvv)
        lru = {}

        def lru_conv(c):
            xc, bxc = poolF.get()
            cw = lambda j, c=c: vecs[:, l, V_CW + c * 4 + j:V_CW + c * 4 + j + 1]
            act(xc, uxb[:, c, 3:3 + TT], AF.Identity, [b_ux[c], b_vecs], [bxc], scale=cw(3), bias=vecs[:, l, V_CB + c:V_CB + c + 1])
            for j in (2, 1, 0):
                stt(xc, uxb[:, c, j:j + TT], cw(j), xc, ALU.mult, ALU.add, [b_ux[c], b_vecs, bxc], [bxc])
            xcb, bxcb = xcb_t[:], b_xcbt
            cp("pool", xcb, xc, [bxc], [bxcb])
            lru[c] = (xc, bxc, xcb, bxcb)

        def lru_gates(c):
            xc, bxc, xcb, bxcb = lru[c]
            pr, pi_ = PS.get(), PS.get()
            mm(banks[pr][:, 0:TT], sbw[:, 768 + c * 128:768 + (c + 1) * 128], xcb, True, True, [bsb, bxcb], [b_bank[pr]])
            mm(banks[pi_][:, 0:TT], sbw[:, 1152 + c * 128:1152 + (c + 1) * 128], xcb, True, True, [bsb, bxcb], [b_bank[pi_]])
            lru[c] = (xc, bxc, pr, pi_)

        def lru_rest(c):
            xc, bxc, pr, pi_ = lru[c]
            rr, brr = poolF.get()
            ii, bii = poolF.get()
            act(rr, banks[pr][:, 0:TT], AF.Sigmoid, [b_bank[pr], b_vecs], [brr], bias=vecs[:, l, V_BA + c:V_BA + c + 1])
            yield
            act(ii, banks[pi_][:, 0:TT], AF.Sigmoid, [b_bank[pi_], b_vecs], [bii], bias=vecs[:, l, V_BI + c:V_BI + c + 1])
            PS.put(pr)
            PS.put(pi_)
            yield
            tt("pool", ii, ii, xc, ALU.mult, [bii, bxc], [bii])
            aa, baa = poolF.get()
            s2_, bs2_ = poolF.get()
            act(aa, rr, AF.Exp, [brr, b_cneg], [baa], scale=cneg[:, l, c:c + 1])
            yield
            act(s2_, rr, AF.Exp, [brr, b_cneg], [bs2_], scale=cneg[:, l, 3 + c:4 + c])
            yield
            ts("dve", s2_, s2_, -1.0, 1.0, ALU.mult, ALU.add, [bs2_], [bs2_])
            yield
            act(s2_, s2_, AF.Ln, [bs2_], [bs2_])
            yield
            act(s2_, s2_, AF.Exp, [bs2_], [bs2_], scale=0.5)
            yield
            tt("pool", ii, ii, s2_, ALU.mult, [bii, bs2_], [bii])
            yield
            P.op("dve", lambda e, rr=rr, aa=aa, ii=ii, c=c: e.tensor_tensor_scan(out=rr, data0=aa, data1=ii, initial=carry[:, c:c + 1],
                                                                                op0=ALU.mult, op1=ALU.add),
                 [baa, bii, b_carry[c], brr], [brr])
            yield
            cp("act", carry[:, c:c + 1], rr[:, TT - 1:TT], [brr], [b_carry[c]])
            tt("pool", mixC[:, c, :], rr, gel[:, c, :], ALU.mult, [brr, b_gel[c]], [b_mixC[c]])

        att_scale = float((NOPE + ROPE) ** -0.5)

        def qproj(h):
            qb = h % 2
            pA, pB = PS.get(), PS.get()
            for kc in range(2):
                mm(banks[pA][0:96, 0:TT], sa[:, kc * 576 + h * 96:kc * 576 + h * 96 + 96], cqn[:, kc, :], kc == 0, kc == 1,
                   [bsa, b_cqn[kc]], [b_bank[pA]])
            for kc in range(2):
                o = 1152 + (kc * HM + h) * 32
                mm(banks[pB][0:96, 0:TT], sa[:, o - 64:o + 32], cqn[:, kc, :], kc == 0, kc == 1, [bsa, b_cqn[kc]], [b_bank[pB]])
            cp("act", qTall[0:64, qb, :], banks[pA][0:64, 0:TT], [b_bank[pA]], [b_qT[qb]])
            rot(pA, pB, slice(64, 96),
                lambda t1, t2, bl, qb=qb: tt("pool", qTall[64:96, qb, :], t1[64:96, :], t2[64:96, :], ALU.add, bl, [b_qT[qb]]))
            PS.put(pA)
            PS.put(pB)

        def kloop(h, filler=None):
            qb = h % 2
            po = PS.get()
            nkb = (t + 1) * BPT

            def qk(kb):
                o = kb - t * BPT
                c0 = 0 if o < 0 else o * 128
                tk = kb // BPT
                ps_ = PS.get()
                mm(banks[ps_][:, c0:TT], kT[0:96, h, kb * 128:(kb + 1) * 128], qTall[0:96, qb, c0:TT], True, True,
                   [b_kT[h][tk], b_qT[qb]], [b_bank[ps_]])
                return ps_, o, c0, tk
            ahead = [qk(kb) for kb in range(min(2, nkb))]
            for kb in range(nkb):
                ps_, o, c0, tk = ahead.pop(0)
                if kb + 2 < nkb:
                    ahead.append(qk(kb + 2))
                pt, bpt = poolH.get()
                act(pt[:, c0:TT], banks[ps_][:, c0:TT], AF.Exp, [b_bank[ps_]], [bpt], scale=att_scale)
                PS.put(ps_)
                if o >= 0:
                    tt("dve", pt[:, c0:c0 + 128], pt[:, c0:c0 + 128], trib[:], ALU.mult, [bpt, b_trib], [bpt])
                mm(banks[po][0:65, c0:TT], Vc[:, kb, h * 65:(h + 1) * 65], pt[:, c0:TT], kb == 0, kb == nkb - 1,
                   [b_V[tk], bpt], [b_bank[po]])
                if filler is not None:
                    next(filler, None)
            if filler is not None:
                for _ in filler:
                    pass
            rd, brd = rdens[h % 2], b_rdens[h % 2]
            P.op("dve", lambda e, po=po, rd=rd: e.reciprocal(out=rd[64:65, :], in_=banks[po][64:65, 0:TT]), [b_bank[po]], [brd])
            return po

        def atail(h, po):
            rd, brd = rdens[h % 2], b_rdens[h % 2]
            pbc = PS.get()
            mm(banks[pbc][0:64, 0:TT], ones_f[64:65, 0:64], rd[64:65, :], True, True, [b_onesf, brd], [b_bank[pbc]])
            cp("act", bcs_ap[0:64, :], banks[pbc][0:64, 0:TT], [b_bank[pbc]], [b_bcs])
            PS.put(pbc)
            tt("dve", mixA[:, h, :], banks[po][0:64, 0:TT], bcs_ap[0:64, :], ALU.mult, [b_bank[po], b_bcs], [b_mixA[h]])
            PS.put(po)

        qproj(0)
        pos_ = {}
        for h in range(HM):
            c = h // 2
            if h % 2 == 0:
                lru_conv(c)
                pos_[h] = kloop(h)
                lru_gates(c)
            else:
                pos_[h] = kloop(h, lru_rest(c))
            if h + 1 < HM:
                qproj(h + 1)
            if h >= 1:
                atail(h - 1, pos_.pop(h - 1))
        atail(HM - 1, pos_.pop(HM - 1))
        for c in range(DC):
            sw_, bsw = wget()
            py = PS.get()
            for h in range(HM):
                mm(banks[py][:, 0:TT], sw_[0:64, h * 128:(h + 1) * 128], mixA[:, h, :], h == 0, False, [bsw, b_mixA[h]], [b_bank[py]])
            for i in range(2):
                mm(banks[py][:, 0:TT], sw_[:, 768 + i * 128:768 + (i + 1) * 128], mixB[:, i, :], False, False, [bsw, b_mixB[i]], [b_bank[py]])
            for i in range(3):
                mm(banks[py][:, 0:TT], sw_[:, 768 + (2 + i) * 128:768 + (3 + i) * 128], mixC[:, i, :], False, i == 2,
                   [bsw, b_mixC[i]], [b_bank[py]])
            stt(xT[:, c, sl], banks[py][:, 0:TT], modcol(l, 2, c, bq), xT[:, c, sl], ALU.mult, ALU.add,
                [b_bank[py], b_modraw, b_xT[c][t]], [b_xT[c][t]])
            PS.put(py)
        rms_mod(t, lambda c: modA[:, l * 2 + 1, c, bq:bq + 1], lambda c: modcol(l, 3, c, bq), lambda c: hT[:, c, :], b_hT)
        for jj in range(NJ // 2):
            sg_, bsg_ = wget()
            su_, bsu_ = wget()
            for sub in range(2):
                j = jj * 2 + sub
                pg, pu = PS.get(), PS.get()
                for kc in range(DC):
                    mm(banks[pg][:, 0:TT], v3(sg_)[:, kc, sub * 128:(sub + 1) * 128], hT[:, kc, :], kc == 0, kc == DC - 1,
                       [bsg_, b_hT[kc]], [b_bank[pg]])
                    mm(banks[pu][:, 0:TT], v3(su_)[:, kc, sub * 128:(sub + 1) * 128], hT[:, kc, :], kc == 0, kc == DC - 1,
                       [bsu_, b_hT[kc]], [b_bank[pu]])
                ft, bft = ffn_tmp.get()
                act(ft, banks[pg][:, 0:TT], AF.Silu, [b_bank[pg]], [bft])
                PS.put(pg)
                tt("dve", arena[:, j * TT:(j + 1) * TT], banks[pu][:, 0:TT], ft, ALU.mult, [b_bank[pu], bft],
                   [b_act[j]] + ([b_rden2] if j >= 20 else []))
                PS.put(pu)
        for c in range(DC):
            d0, bd0 = wget()
            d1, bd1 = wget()
            py = PS.get()
            for j in range(NJ):
                dsl, bd = (d0, bd0) if j < 11 else (d1, bd1)
                jj = j % 11
                mm(banks[py][:, 0:TT], dsl[:, jj * 128:(jj + 1) * 128], arena[:, j * TT:(j + 1) * TT], j == 0, j == NJ - 1,
                   [bd, b_act[j]] + ([b_rden2] if j >= 20 else []), [b_bank[py]])
            stt(xT[:, c, sl], banks[py][:, 0:TT], modcol(l, 5, c, bq), xT[:, c, sl], ALU.mult, ALU.add,
                [b_bank[py], b_modraw, b_xT[c][t]], [b_xT[c][t]])
            PS.put(py)

    for bq in range(NSEQ):
        for t in range(NT):
            sl = slice(t * TT, (t + 1) * TT)
            pi, bpi = poolF.get()
            pos_i = pi.bitcast(I32)
            ld(pos_i, pos_d[bq:bq + 1, sl].partition_broadcast(128), [], [bpi], bpi)
            ang, bang = poolF.get()
            cp("dve", ang, pos_i, [bpi], [bang])
            ts("dve", ang, ang, consts[:, C_INVF:C_INVF + 1], None, ALU.mult, ALU.bypass, [bang, b_const], [bang])
            for which, shift in ((1, 0.0), (0, math.pi / 2)):
                u, bu = poolF.get()
                ki, bki = poolF.get()
                ki_i = ki.bitcast(I32)
                ts("dve", u, ang, 1.0 / TWO_PI, 0.5 + shift / TWO_PI, ALU.mult, ALU.add, [bang], [bu])
                cp("dve", ki_i, u, [bu], [bki])
                cp("dve", u, ki_i, [bki], [bu])
                stt(u, u, -TWO_PI, ang, ALU.mult, ALU.add, [bu, bang], [bu])
                if shift:
                    ts("dve", u, u, shift, None, ALU.add, ALU.bypass, [bu], [bu])
                ts("dve", ki, u, math.pi, -TWO_PI, ALU.is_gt, ALU.mult, [bu], [bki])
                tt("dve", u, u, ki, ALU.add, [bu, bki], [bu])
                ts("dve", ki, u, -math.pi, TWO_PI, ALU.is_lt, ALU.mult, [bu], [bki])
                tt("dve", u, u, ki, ALU.add, [bu, bki], [bu])
                act(ki, u, AF.Sin, [bu], [bki])
                P.dma(lambda e, ki=ki, which=which, sl=sl, bq=bq: e.dma_start(out=cs_scr[bq, which, :, sl], in_=ki), reads=[bki],
                      writes=[b_csscr[bq]], sem_buf=bki)
        for blk in range(NB):
            t = blk // BPT
            xs, bxs = xstg.get()
            ld(xs, x_d[bq, blk * 128:(blk + 1) * 128, :], [], [bxs], bxs)
            for half in range(2):
                pb = PS.get()
                for cc in range(4):
                    c = half * 4 + cc
                    mm(banks[pb][:, cc * 128:(cc + 1) * 128], xs[:, c * 128:(c + 1) * 128], ident_f, True, True, [bxs, b_const], [b_bank[pb]])
                cp("act" if half else "dve", xT[:, half * 4:half * 4 + 4, blk * 128:(blk + 1) * 128],
                   banks[pb][:, 0:512].rearrange("p (c n) -> p c n", n=128), [b_bank[pb]], [b_xT[half * 4 + cc][t] for cc in range(4)])
                PS.put(pb)
        P.barrier()
        for l in range(L):
            ld(gnbc[:], gn_d[l:l + 1, :].partition_broadcast(128), [], [b_gn], b_gn)
            P.op("pool", lambda e: e.memset(stf[:], 0.0), [], b_stfh)
            P.op("pool", lambda e: e.memset(stb[:], 0.0), [], [b_stb])
            P.op("pool", lambda e: e.memset(carry[:], 0.0), [], b_carry)
            P.op("pool", lambda e: e.memset(uhist[:], 0.0), [], b_uh)
            for t in range(NT):
                tile_body(bq, l, t)
        P.barrier()
        for t in range(NT):
            rms_mod(t, lambda c: modA[:, L * 2, c, bq:bq + 1], lambda c: modraw[:, L * 48 + c, bq:bq + 1],
                    lambda c, t=t: xT[:, c, t * TT:(t + 1) * TT], [b_xT[c][t] for c in range(DC)])
            for n in range(BPT):
                blk = t * BPT + n
                xs, bxs = xstg.get()
                for half in range(2):
                    pb = PS.get()
                    for cc in range(4):
                        c = half * 4 + cc
                        mm(banks[pb][:, cc * 128:(cc + 1) * 128], xT[:, c, blk * 128:(blk + 1) * 128], ident_f, True, True,
                           [b_xT[c][t], b_const], [b_bank[pb]])
                    cp("act" if half else "dve", xs[:, half * 512:(half + 1) * 512], banks[pb][:, 0:512], [b_bank[pb]], [bxs])
                    PS.put(pb)
                out_stores.append(P.dma(lambda e, xs=xs, blk=blk, bq=bq: e.dma_start(out=out_d[bq, blk * 128:(blk + 1) * 128, :], in_=xs),
                                        reads=[bxs], sem_buf=bxs))
        P.barrier()
    P.wait_all("sp", out_stores)
    P.emit()
    print("[build] sbuf bytes remaining/partition:", nc.sbuf_bytes_remaining, " ops:", {e: len(v) for e, v in P.ops.items()}, flush=True)
    return nc


def pack_inputs(inp, cfg, n_cores):
    L, NSEQ = cfg.L, cfg.NSEQ
    f = lambda k: np.ascontiguousarray(np.asarray(inp[k]))
    vecs = np.zeros((128, L, NV), np.float32)

    def colmaj(v, n):
        return np.asarray(v, np.float32).reshape(n, 128).T

    for l in range(L):
        vecs[:, l, V_N1:V_N1 + 8] = colmaj(inp["norm1"][l], 8)
        vecs[:, l, V_N2:V_N2 + 8] = colmaj(inp["norm2"][l], 8)
        vecs[:, l, V_QN:V_QN + 2] = colmaj(inp["mla_q_norm"][l], 2)
        vecs[:, l, V_KVN:V_KVN + 1] = colmaj(inp["mla_kv_norm"][l], 1)
        cw = np.asarray(inp["lru_conv_w"][l], np.float32)
        for c in range(3):
            for j in range(4):
                vecs[:, l, V_CW + c * 4 + j] = cw[j, c * 128:(c + 1) * 128]
        vecs[:, l, V_CB:V_CB + 3] = colmaj(inp["lru_conv_b"][l], 3)
        vecs[:, l, V_BA:V_BA + 3] = colmaj(inp["lru_b_a"][l], 3)
        vecs[:, l, V_BI:V_BI + 3] = colmaj(inp["lru_b_i"][l], 3)
        vecs[:, l, V_LAM:V_LAM + 3] = colmaj(inp["lru_lambda"][l], 3)
        vecs[:, l, V_MB:V_MB + 48] = colmaj(inp["mod_b"][l], 48)
    fvec = np.zeros((128, 24), np.float32)
    fvec[:, 0:8] = colmaj(inp["final_norm"], 8)
    fvec[:, 8:24] = colmaj(inp["final_mod_b"], 16)
    shared = {
        "mod_w": f("mod_w"), "w_in": f("w_in"), "w_uq": f("mla_w_uq"), "w_ukv": f("mla_w_ukv"),
        "lru_w_a": f("lru_w_a"), "lru_w_i": f("lru_w_i"), "w_out": f("w_out"), "w_gu": f("w_gate_up"),
        "w_down": f("w_down"), "fmod_w": f("final_mod_w"), "vecs": vecs, "fvec": fvec, "ret_gn": f("ret_gn"),
        "consts": make_consts(),
    }
    x, c, pos = np.asarray(inp["x"]), np.asarray(inp["c"]), np.asarray(inp["positions"])
    maps = []
    for i in range(n_cores):
        sl = slice(i * NSEQ, (i + 1) * NSEQ)
        m = dict(shared)
        m["x"] = np.ascontiguousarray(x[sl])
        m["cT"] = np.ascontiguousarray(c[sl].reshape(NSEQ, 8, 128).transpose(2, 1, 0))
        m["pos"] = np.ascontiguousarray(pos[sl].astype(np.int32))
        maps.append(m)
    return maps


_NC_CACHE = {}
SEQ_PER_LAUNCH = 2


def kernel(**inputs):
    n_cores = 8
    nseq = SEQ_PER_LAUNCH
    cfg = Cfg(NSEQ=nseq)
    if nseq not in _NC_CACHE:
        _NC_CACHE[nseq] = build_program(cfg)
    nc = _NC_CACHE[nseq]
    B = np.asarray(inputs["x"]).shape[0]
    outs = []
    per = n_cores * nseq
    for part in range(B // per):
        sub = dict(inputs)
        sl = slice(part * per, (part + 1) * per)
        sub["x"] = np.asarray(inputs["x"])[sl]
        sub["c"] = np.asarray(inputs["c"])[sl]
        sub["positions"] = np.asarray(inputs["positions"])[sl]
        maps = pack_inputs(sub, cfg, n_cores)
        res = run_bass_kernel_spmd(nc, maps, core_ids=list(range(n_cores)))
        outs += [np.asarray(r["out"]) for r in res.results]
    return np.concatenate(outs, axis=0).astype(np.float32)
```
